# Optimizing a Trainium2 kernel written in Bass

```python
import jax, jax.numpy as jnp
from jax import lax
import numpy as np

D_MODEL = 2048
BATCH = 4
SEQ = 2048
DEPTH = 4
DEC_BATCH = 128
DEC_SEQ = 8
PAST_LEN = 16384
PAGE_SIZE = 128

POOL_WIDTH = D_MODEL // 4
POOL_WINDOWS = (2, 4, 8, 16)
POOL_GROUPS = len(POOL_WINDOWS)
POOL_GROUP_DIM = POOL_WIDTH // POOL_GROUPS
POOL_HIST = max(POOL_WINDOWS) - 1
GMLP_WIDTH = D_MODEL // 2
GMLP_HEADS = 8
GMLP_HEAD_DIM = GMLP_WIDTH // GMLP_HEADS
CHUNK = 128
CONV_WIDTH = D_MODEL // 4
CONV_K = 3
CONV_HIST = CONV_K - 1
MIX_WIDTH = POOL_WIDTH + GMLP_WIDTH + CONV_WIDTH
IN_WIDTH = POOL_WIDTH + 2 * GMLP_WIDTH + 3 * CONV_WIDTH
IN_SPLITS = (POOL_WIDTH, POOL_WIDTH + GMLP_WIDTH, POOL_WIDTH + 2 * GMLP_WIDTH,
             POOL_WIDTH + 2 * GMLP_WIDTH + CONV_WIDTH, POOL_WIDTH + 2 * GMLP_WIDTH + 2 * CONV_WIDTH)
D_FF = 4 * D_MODEL
ALPHA = (2 * DEPTH) ** 0.25
BETA = (8 * DEPTH) ** -0.25
LN_EPS = 1e-5

kernel_name = 'hybrid_pool_gmlp_conv_decoder_step'


def _layernorm(x, g, b):
    xf = x.astype(jnp.float32)
    mu = jnp.mean(xf, axis=-1, keepdims=True)
    var = jnp.mean(jnp.square(xf - mu), axis=-1, keepdims=True)
    y = (xf - mu) * lax.rsqrt(var + LN_EPS) * g.astype(jnp.float32) + b.astype(jnp.float32)
    return y.astype(x.dtype)


def _pool_mixer(p, p_hist, pos0, w_pool, pool_scale):
    L = p.shape[1]
    ext = jnp.concatenate([p_hist, p], axis=1)
    ext32 = ext.astype(jnp.float32)
    csum = jnp.concatenate([jnp.zeros_like(ext32[:, :1]), jnp.cumsum(ext32, axis=1)], axis=1)
    end = csum[:, POOL_HIST + 1:POOL_HIST + 1 + L]
    pos = pos0 + jnp.arange(L)
    means = []
    for g, w in enumerate(POOL_WINDOWS):
        cs = slice(g * POOL_GROUP_DIM, (g + 1) * POOL_GROUP_DIM)
        start = csum[:, POOL_HIST + 1 - w:POOL_HIST + 1 - w + L, cs]
        count = jnp.minimum(w, pos + 1).astype(jnp.float32)[None, :, None]
        means.append((end[..., cs] - start) / count)
    pooled = jnp.concatenate(means, axis=-1).astype(p.dtype)
    d = (pooled - p).reshape(p.shape[0], L, POOL_GROUPS, POOL_GROUP_DIM)
    out = jnp.einsum('blgc,gce->blge', d, w_pool).reshape(p.shape[0], L, POOL_WIDTH)
    return out * pool_scale, ext[:, -POOL_HIST:]


def _spatial_gate(u, v, w_s, b_s):
    B_, L, _ = v.shape
    Lp = -(-L // CHUNK) * CHUNK
    vp = v if Lp == L else jnp.pad(v, ((0, 0), (0, Lp - L), (0, 0)))
    v5 = vp.reshape(B_, Lp // CHUNK, CHUNK, GMLP_HEADS, GMLP_HEAD_DIM)
    mask = jnp.tril(jnp.ones((CHUNK, CHUNK), w_s.dtype))
    mixed = jnp.einsum('hij,bcjhd->bcihd', w_s * mask, v5) + jnp.transpose(b_s)[:, :, None]
    mixed = mixed.reshape(B_, Lp, GMLP_WIDTH)[:, :L]
    return u * mixed


def _short_conv(z, z_hist, w_conv):
    L = z.shape[1]
    ext = jnp.concatenate([z_hist, z], axis=1)
    y = w_conv[0] * ext[:, 0:L]
    for k in range(1, CONV_K):
        y = y + w_conv[k] * ext[:, k:k + L]
    return y, ext[:, -CONV_HIST:]


def _layer(x, pool_hist, conv_hist, pos0, w_in, w_pool, pool_scale, ln_v_g, ln_v_b, w_s, b_s,
           w_conv, w_out, ln1_g, ln1_b, w_ff1, w_ff2, ln2_g, ln2_b):
    proj = jnp.einsum('bld,de->ble', x, w_in)
    p, u, v, gate_b, gate_c, z = jnp.split(proj, IN_SPLITS, axis=-1)
    a_out, pool_new = _pool_mixer(p, pool_hist, pos0, w_pool, pool_scale)
    u = jax.nn.gelu(u)
    v = _layernorm(jax.nn.gelu(v), ln_v_g, ln_v_b)
    b_out = _spatial_gate(u, v, w_s, b_s)
    conv_y, conv_new = _short_conv(gate_c * z, conv_hist, w_conv)
    c_out = gate_b * conv_y
    mix = jnp.concatenate([a_out, b_out, c_out], axis=-1)
    h = _layernorm(ALPHA * x + jnp.einsum('ble,ed->bld', mix, w_out), ln1_g, ln1_b)
    f = jnp.einsum('blf,fd->bld', jnp.square(jax.nn.relu(jnp.einsum('bld,df->blf', h, w_ff1))), w_ff2)
    x = _layernorm(ALPHA * h + f, ln2_g, ln2_b)
    return x, pool_new, conv_new, v


def setup_inputs(seed: int = 0) -> dict:
    key = jax.random.key(seed)
    ks = jax.random.split(key, 20)
    nrm = jax.random.normal
    return {
        'x_prompt': nrm(ks[0], (BATCH, SEQ, D_MODEL), jnp.float32),
        'x_sample': nrm(ks[1], (DEC_BATCH, DEC_SEQ, D_MODEL), jnp.float32),
        'state_pool': nrm(ks[2], (DEPTH, DEC_BATCH, POOL_HIST, POOL_WIDTH), jnp.float32),
        'state_conv': nrm(ks[3], (DEPTH, DEC_BATCH, CONV_HIST, CONV_WIDTH), jnp.float32),
        'w_in': nrm(ks[4], (DEPTH, D_MODEL, IN_WIDTH), jnp.float32) * D_MODEL ** -0.5,
        'w_pool': nrm(ks[5], (DEPTH, POOL_GROUPS, POOL_GROUP_DIM, POOL_GROUP_DIM), jnp.float32) * POOL_GROUP_DIM ** -0.5,
        'pool_scale': 1.0 + 0.1 * nrm(ks[6], (DEPTH, POOL_WIDTH), jnp.float32),
        'ln_v_g': 1.0 + 0.05 * nrm(ks[7], (DEPTH, GMLP_WIDTH), jnp.float32),
        'ln_v_b': 0.02 * nrm(ks[8], (DEPTH, GMLP_WIDTH), jnp.float32),
        'w_s': nrm(ks[9], (DEPTH, GMLP_HEADS, CHUNK, CHUNK), jnp.float32) * CHUNK ** -0.5,
        'b_s': 1.0 + 0.1 * nrm(ks[10], (DEPTH, GMLP_HEADS, CHUNK), jnp.float32),
        'w_conv': nrm(ks[11], (DEPTH, CONV_K, CONV_WIDTH), jnp.float32) * CONV_K ** -0.5,
        'w_out': nrm(ks[12], (DEPTH, MIX_WIDTH, D_MODEL), jnp.float32) * (MIX_WIDTH ** -0.5 * BETA),
        'ln1_g': 1.0 + 0.05 * nrm(ks[13], (DEPTH, D_MODEL), jnp.float32),
        'ln1_b': 0.02 * nrm(ks[14], (DEPTH, D_MODEL), jnp.float32),
        'w_ff1': nrm(ks[15], (DEPTH, D_MODEL, D_FF), jnp.float32) * D_MODEL ** -0.5,
        'w_ff2': nrm(ks[16], (DEPTH, D_FF, D_MODEL), jnp.float32) * (D_FF ** -0.5 * BETA),
        'ln2_g': 1.0 + 0.05 * nrm(ks[17], (DEPTH, D_MODEL), jnp.float32),
        'ln2_b': 0.02 * nrm(ks[18], (DEPTH, D_MODEL), jnp.float32),
    }


def reference(x_prompt, x_sample, state_pool, state_conv, w_in, w_pool, pool_scale, ln_v_g, ln_v_b,
              w_s, b_s, w_conv, w_out, ln1_g, ln1_b, w_ff1, w_ff2, ln2_g, ln2_b):
    xp, xs = x_prompt, x_sample
    nb = xp.shape[0]
    pool_p, conv_p, pool_s, conv_s, chunk_v_s = [], [], [], [], []
    for l in range(DEPTH):
        wl = (w_in[l], w_pool[l], pool_scale[l], ln_v_g[l], ln_v_b[l], w_s[l], b_s[l], w_conv[l],
              w_out[l], ln1_g[l], ln1_b[l], w_ff1[l], w_ff2[l], ln2_g[l], ln2_b[l])
        hist_pool0 = jnp.zeros((nb, POOL_HIST, POOL_WIDTH), xp.dtype)
        hist_conv0 = jnp.zeros((nb, CONV_HIST, CONV_WIDTH), xp.dtype)
        xp, pn, cn, _ = _layer(xp, hist_pool0, hist_conv0, 0, *wl)
        pool_p.append(pn)
        conv_p.append(cn)
        xs, pn, cn, vs = _layer(xs, state_pool[l], state_conv[l], PAST_LEN, *wl)
        pool_s.append(pn)
        conv_s.append(cn)
        chunk_v_s.append(vs)
    return (xp, xs, jnp.stack(pool_p), jnp.stack(conv_p), jnp.stack(pool_s), jnp.stack(conv_s), jnp.stack(chunk_v_s))
```

```python
import bisect
from contextlib import ExitStack

import numpy as np
import concourse.bass as bass
import concourse.mybir as mybir
from concourse.bass_utils import run_bass_kernel_spmd

F32 = mybir.dt.float32
BF16 = mybir.dt.bfloat16
AF = mybir.ActivationFunctionType
ALU = mybir.AluOpType

DEPTH = 4
D = 2048
NT = 6
T = NT * 128
TG = 384
ALPHA = (2 * DEPTH) ** 0.25
EPS = 1e-5
WINDOWS = (2, 4, 8, 16)
NSLOT = 2


class Eng:
    def __init__(self, h, sem, name):
        self.h, self.sem, self.name = h, sem, name
        self.instrs = []
        self.inc_idx = []
        self.inc_val = {}
        self.n_inc = 0
        self.waited = {}
        self.eager = False

    def ensure_inc(self, idx):
        k = bisect.bisect_left(self.inc_idx, idx)
        if k < len(self.inc_idx):
            return self.inc_val[self.inc_idx[k]]
        self.n_inc += 1
        self.instrs[idx].then_inc(self.sem, 1)
        self.inc_idx.append(idx)
        self.inc_val[idx] = self.n_inc
        return self.n_inc

    def wait_for(self, dep):
        if dep[0] == 'e':
            eng, idx = dep[1], dep[2]
            sem, val, key = eng.sem, eng.ensure_inc(idx), eng.name
        else:
            ds, val = dep[1], dep[2]
            sem, key = ds.sem, ds.name
        if self.waited.get(key, 0) >= val:
            return
        self.h.wait_ge(sem, val)
        self.waited[key] = val


class DSem:
    def __init__(self, sem, name):
        self.sem, self.name, self.count = sem, name, 0


class Buf:
    __slots__ = ('name', 'w', 'r')

    def __init__(self, name=''):
        self.name, self.w, self.r = name, None, {}


def _deps(reads, writes, acc):
    deps = []
    for b in reads:
        if b.w is not None:
            deps.append(b.w)
    if not acc:
        for b in writes:
            if b.r:
                deps.extend(b.r.values())
            elif b.w is not None:
                deps.append(b.w)
    return deps


def op(eng, fn, reads=(), writes=(), acc=False):
    for d in _deps(reads, writes, acc):
        eng.wait_for(d)
    ins = fn()
    idx = len(eng.instrs)
    eng.instrs.append(ins)
    if eng.eager:
        eng.ensure_inc(idx)
    tok = ('e', eng, idx)
    for b in reads:
        b.r[eng.name] = tok
    for b in writes:
        b.w = tok
        if not acc:
            b.r = {}
    return tok


def dma(q, out, in_, ds, reads=(), writes=()):
    for d in _deps(reads, writes, False):
        q.wait_for(d)
    q.h.dma_start(out=out, in_=in_).then_inc(ds.sem, 16)
    ds.count += 16
    tok = ('d', ds, ds.count)
    for b in reads:
        b.r[ds.name] = tok
    for b in writes:
        b.w = tok
        b.r = {}
    return tok


def _tok_key(tok):
    return tok[1].name


def _tok_later(a, b):
    return a if a[2] >= b[2] else b


def inherit(news, olds):
    merged = {}
    for o in olds:
        toks = list(o.r.values())
        if o.w is not None:
            toks.append(o.w)
        for t in toks:
            k = _tok_key(t)
            merged[k] = _tok_later(merged[k], t) if k in merged else t
    for n in news:
        n.r = dict(merged)
        n.w = None


class Rot:
    def __init__(self, items):
        self.items, self.i = items, 0

    def next(self):
        it = self.items[self.i % len(self.items)]
        self.i += 1
        return it


def build(depth=DEPTH, npass=2):
    nc = bass.Bass("TRN2", target_bir_lowering=False)

    def din(name, shape):
        return nc.dram_tensor(name, list(shape), F32, kind="ExternalInput").ap()

    def dout(name, shape):
        return nc.dram_tensor(name, list(shape), F32, kind="ExternalOutput").ap()

    x_in = din("x_in", [12, 128, D])
    flag_d = din("flag", [128, 1])
    rc0_d = din("rc0", [128, 64])
    st_pool = din("st_pool", [depth, 240, 512])
    st_conv = din("st_conv", [depth, 32, 512])
    w_in = din("w_in", [depth, D, 4096])
    w_out = din("w_out", [depth, D, D])
    w_ff1 = din("w_ff1", [depth, D, 8192])
    w_ff2 = din("w_ff2", [depth, 8192, D])
    w_pool = din("w_pool", [depth, 4, 128, 128])
    pscale_d = din("pscale", [depth, 128, 4])
    wconv_d = din("wconv", [depth, 128, 12])
    lnv_g = din("lnv_g", [depth, 1024])
    lnv_b = din("lnv_b", [depth, 1024])
    wsT_d = din("wsT", [depth, 8, 128, 128])
    wsTp_d = din("wsTp", [depth, 8, 128, 128])
    bs_d = din("bs", [depth, 8, 128])
    bss_d = din("bss", [depth, 8, 128])
    sel_d = din("sel", [8, 1024])
    ln1_g = din("ln1_g", [depth, D])
    ln1_b = din("ln1_b", [depth, D])
    ln2_g = din("ln2_g", [depth, D])
    ln2_b = din("ln2_b", [depth, D])
    ident_d = din("ident", [128, 128])
    tril_d = din("tril", [128, 128])
    blk_d = din("blk", [128, 128])

    y_d = dout("y", [9, 128, D])
    ptm_p_d = dout("ptm_p", [depth, 128, 512])
    cztm_p_d = dout("cztm_p", [depth, 128, 512])
    ptm_s_d = dout("ptm_s", [depth, 128, 512])
    cztm_s_d = dout("cztm_s", [depth, 128, 512])
    hist_s_d = dout("hist_s", [depth, 16, 7, 512])
    chunkv_d = dout("chunkv", [depth, 128, 1024])

    with ExitStack() as es:
        def sb(name, shape, dt=F32):
            return es.enter_context(nc.sbuf_tensor(name, list(shape), dt))

        def ps(name, shape, dt=F32):
            return es.enter_context(nc.psum_tensor(name, list(shape), dt))

        def sem(name):
            return es.enter_context(nc.semaphore(name))

        PE = Eng(nc.tensor, sem("s_pe"), "pe")
        ACT = Eng(nc.scalar, sem("s_act"), "act")
        DVE = Eng(nc.vector, sem("s_dve"), "dve")
        POOL = Eng(nc.gpsimd, sem("s_pool"), "pool")
        SP = Eng(nc.sync, sem("s_sp"), "sp")
        ACT.eager = True
        DVE.eager = True

        def dsem(name):
            return DSem(sem(name), name)

        X = sb("X", [128, NT, D])
        Xb = [[Buf() for _ in range(4)] for _ in range(NT)]
        Xds = [dsem(f"xds{t}") for t in range(NT)]
        actT = sb("actT", [128, 16, T], BF16)
        actb = [Buf() for _ in range(NT)]
        mixT = sb("mixT", [128, 16, T], BF16)
        mixb = [[Buf() for _ in range(NT)] for _ in range(16)]
        vtm = sb("vtm", [128, NT, 1024], BF16)
        vtb = [Buf() for _ in range(NT)]
        wsl = [sb(f"wsl{i}", [128, 16, 512], BF16) for i in range(NSLOT)]
        wslb = [Buf() for _ in range(NSLOT)]
        wsld = [dsem(f"wsld{i}") for i in range(NSLOT)]
        R = sb("R", [128, 6160])
        xb16 = [sb(f"xb16_{i}", [128, D], BF16) for i in range(2)]
        xb16b = [Buf(), Buf()]
        dd = sb("dd", [128, T], BF16)
        ddb = Buf()
        ug = [sb(f"ug{i}", [128, T], BF16) for i in range(2)]
        ugb = [Buf() for _ in range(2)]
        rl = [sb(f"rl{i}", [128, 512]) for i in range(2)]
        rlb = [Buf() for _ in range(2)]
        wst = sb("wst", [128, 8, 128], BF16)
        wstb = Buf()
        WT = sb("WT", [128, 8, 128], BF16)
        WTb = Buf()
        WTs = sb("WTs", [128, 8, 128], BF16)
        WTsb = Buf()
        bsf = sb("bsf", [8, 256])
        bsfb = Buf()
        bsh = sb("bsh", [8, 256], BF16)
        bsl = sb("bsl", [8, 256], BF16)
        bshb = Buf()
        sel = sb("sel_t", [8, 1024], BF16)
        ones1b = Buf()
        wpool = sb("wpool_t", [128, 4, 128], BF16)
        wpoolb = Buf()
        wpoold = dsem("wpoold")
        pscale = sb("pscale_t", [128, 4])
        wconv = sb("wconv_t", [128, 12])
        smallb = Buf()
        identf = sb("identf", [128, 128])
        identb = sb("identb", [128, 128], BF16)
        tril = sb("tril_t", [128, 128])
        blk = sb("blk_t", [128, 128])
        flag = sb("flagt", [128, 1])
        rc0 = sb("rc0t", [128, 64])
        epst = sb("epst", [128, 1])
        constb = Buf()
        identbb = Buf()
        hsave_p = sb("hsave_p", [128, depth * 4, 15])
        hsave_c = sb("hsave_c", [128, depth * 4, 2])
        hsb = [[Buf() for _ in range(8)] for _ in range(depth)]
        stats = [sb(f"stats{i}", [128, 4, 6]) for i in range(3)]
        statb = [[Buf() for _ in range(4)] for _ in range(3)]
        mvs = [sb(f"mv{i}", [128, 4]) for i in range(3)]
        mvb = [Buf() for _ in range(3)]
        ln_rot = Rot([0, 1, 2])
        mvall = sb("mvall", [128, NT, 4])
        mvallb = [Buf() for _ in range(NT)]
        tmst = [sb(f"tmst{i}", [128, 512]) for i in range(2)]
        tmstb = [Buf() for _ in range(2)]
        tmstd = [dsem(f"tmstd{i}") for i in range(2)]
        stpb = Buf()
        pes = [sb(f"pes{i}", [128, 16, 23]) for i in range(3)]
        pesb = [Buf() for _ in range(3)]
        czs = sb("czs", [128, 4, 16, 10])
        czsb = Buf()
        ysm = sb("ysm", [128, 16, 8])
        ysmb = Buf()
        tmp15 = sb("tmp15", [128, 16])
        tmp15b = Buf()
        wst_ds = dsem("wst_ds")
        bs_ds = dsem("bs_ds")
        st_ds = dsem("st_ds")
        small_ds = dsem("small_ds")
        hist_ds = dsem("hist_ds")
        ln_ds = dsem("ln_ds")
        cst_ds = dsem("cst_ds")
        sel_ds = dsem("sel_ds")
        gv_ds = dsem("gv_ds")

        gv = [R[:, 0:1024], R[:, 3072:4096]]
        gvb = [Buf(), Buf()]
        vg, vbeta = R[:, 1024:2048], R[:, 2048:3072]
        vgb, vbb = Buf(), Buf()
        pe_v, sa_v, sbb_v = R[:, 0:783], R[:, 800:1583], R[:, 1600:2383]
        peb, sab, sbbb = Buf(), Buf(), Buf()
        gcs = [R[:, e * 768:(e + 1) * 768] for e in range(4)]
        gcsb = [Buf() for _ in range(4)]
        cze = [R[:, 3072 + e * 770:3072 + (e + 1) * 770] for e in range(4)]
        czeb = [Buf() for _ in range(4)]
        stp = R[0:120, 2400:3424].rearrange("p (h c) -> p h c", h=2)
        stc = R[0:32, 3424:3936]
        lng, lnb = R[:, 0:2048], R[:, 2048:4096]
        lngb, lnbb = Buf(), Buf()
        allR = gvb + [vgb, vbb, peb, sab, sbbb, stpb] + gcsb + czeb + [lngb, lnbb]

        def switch(news):
            inherit(news, allR)

        pbank = [ps(f"pb{i}", [128, 512]) for i in range(6)]
        pp = Rot([(pbank[i], Buf()) for i in range(6)])
        ptbank = [ps(f"ptb{i}", [128, 8, 128], BF16) for i in range(2)]
        ptp = Rot([(ptbank[i], Buf()) for i in range(2)])

        blocks = []
        for p_ in range(npass):
            for l in range(depth):
                wi = w_in[l].rearrange("(dc p) c -> p dc c", p=128)
                for c0 in (1536, 2048, 512, 1024, 0, 3072, 3584, 2560):
                    blocks.append(wi[:, :, c0:c0 + 512])
                wo = w_out[l].rearrange("(dc p) c -> p dc c", p=128)
                for cb in range(4):
                    blocks.append(wo[:, :, cb * 512:(cb + 1) * 512])
                w1 = w_ff1[l].rearrange("(dc p) c -> p dc c", p=128)
                w2 = w_ff2[l].rearrange("(fc p) c -> p fc c", p=128)
                for s in range(4):
                    for fb in range(4):
                        c0 = s * 2048 + fb * 512
                        blocks.append(w1[:, :, c0:c0 + 512])
                    for db in range(4):
                        blocks.append(w2[:, s * 16:(s + 1) * 16, db * 512:(db + 1) * 512])
        wstate = {'issue': 0, 'use': 0}

        def w_issue_upto(k):
            while wstate['issue'] <= min(k, len(blocks) - 1):
                b = wstate['issue']
                s = b % NSLOT
                dma(POOL, wsl[s][:], blocks[b], wsld[s], writes=[wslb[s]])
                wstate['issue'] += 1

        def w_get(hold=0):
            b = wstate['use']
            w_issue_upto(b + NSLOT - 1 - hold)
            wstate['use'] += 1
            return wsl[b % NSLOT], wslb[b % NSLOT]

        def mm(out, lhsT, rhs, start, stop, reads, bank, first):
            return op(PE, lambda: nc.tensor.matmul(out, lhsT, rhs, start=start, stop=stop),
                      reads=reads, writes=[bank], acc=not first)

        def act(out, in_, func, reads, writes, bias=None, scale=None):
            kw = {}
            if bias is not None:
                kw['bias'] = bias
            if scale is not None:
                kw['scale'] = scale
            return op(ACT, lambda: nc.scalar.activation(out=out, in_=in_, func=func, **kw), reads=reads, writes=writes)

        def tt(out, in0, in1, o, reads, writes):
            return op(DVE, lambda: nc.vector.tensor_tensor(out=out, in0=in0, in1=in1, op=o), reads=reads, writes=writes)

        def ts(out, in0, s1, s2, o0, o1, reads, writes):
            if s2 is None:
                return op(DVE, lambda: nc.vector.tensor_scalar(out=out, in0=in0, scalar1=s1, scalar2=None, op0=o0),
                          reads=reads, writes=writes)
            return op(DVE, lambda: nc.vector.tensor_scalar(out=out, in0=in0, scalar1=s1, scalar2=s2, op0=o0, op1=o1),
                      reads=reads, writes=writes)

        def stt(out, in0, scalar, in1, o0, o1, reads, writes):
            return op(DVE, lambda: nc.vector.scalar_tensor_tensor(out=out, in0=in0, scalar=scalar, in1=in1, op0=o0, op1=o1),
                      reads=reads, writes=writes)

        def pstt(out, in0, scalar, in1, o0, o1, reads, writes):
            return op(POOL, lambda: nc.gpsimd.scalar_tensor_tensor(out=out, in0=in0, scalar=scalar, in1=in1, op0=o0, op1=o1),
                      reads=reads, writes=writes)

        def ptt(out, in0, in1, o, reads, writes):
            return op(POOL, lambda: nc.gpsimd.tensor_tensor(out=out, in0=in0, in1=in1, op=o), reads=reads, writes=writes)

        def pts(out, in0, s1, o0, reads, writes):
            return op(POOL, lambda: nc.gpsimd.tensor_scalar(out=out, in0=in0, scalar1=s1, scalar2=None, op0=o0),
                      reads=reads, writes=writes)

        rl_rot = Rot([0, 1])

        def relu2(out, psum_ap, n, pb, wbufs):
            ri = rl_rot.next()
            act(rl[ri][:, 0:n], psum_ap, AF.Relu, [pb], [rlb[ri]])
            act(out, rl[ri][:, 0:n], AF.Square, [rlb[ri]], wbufs)

        def vcopy(out, in_, reads, writes):
            return op(DVE, lambda: nc.vector.tensor_copy(out=out, in_=in_), reads=reads, writes=writes)

        def vmemset(ap, val, writes):
            return op(DVE, lambda: nc.vector.memset(ap, val), writes=writes)

        def cgroups(c_lo):
            ncol = T - c_lo
            ngrp = 2 if ncol > 512 else 1
            gw = ncol // ngrp
            return [(c_lo + i * gw, c_lo + (i + 1) * gw) for i in range(ngrp)]

        def tiles_of(c0, c1):
            return range(c0 // 128, (c1 + 127) // 128)

        def mixbufs(e, c0, c1):
            return [mixb[e][t] for t in tiles_of(c0, c1)]

        def actbufs(c0, c1):
            return [actb[t] for t in tiles_of(c0, c1)]

        def layernorm(xap, xbufs, W, g_ap, b_ap, gbuf, bbuf):
            nk = W // 512
            i = ln_rot.next()
            st, stb, mv, mb = stats[i], statb[i], mvs[i], mvb[i]
            for k in range(nk):
                op(DVE, lambda k=k: nc.vector.bn_stats(out=st[:, k, :], in_=xap[:, k * 512:(k + 1) * 512]),
                   reads=[xbufs[k]], writes=[stb[k]])
            op(DVE, lambda: nc.vector.bn_aggr(out=mv[:, 0:2], in_=st[:, 0:nk, :].rearrange("p k s -> p (k s)")),
               reads=stb[0:nk], writes=[mb])
            act(mv[:, 2:3], mv[:, 1:2], AF.Sqrt, reads=[mb, constb], writes=[mb], bias=epst[:, 0:1], scale=1.0)
            op(DVE, lambda: nc.vector.reciprocal(out=mv[:, 2:3], in_=mv[:, 2:3]), reads=[mb], writes=[mb])
            for k in range(nk):
                sl = slice(k * 512, (k + 1) * 512)
                stt(xap[:, sl], xap[:, sl], mv[:, 0:1], g_ap[:, sl], ALU.subtract, ALU.mult, [xbufs[k], mb, gbuf], [xbufs[k]])
            for k in range(nk):
                sl = slice(k * 512, (k + 1) * 512)
                stt(xap[:, sl], xap[:, sl], mv[:, 2:3], b_ap[:, sl], ALU.mult, ALU.add, [xbufs[k], mb, bbuf], [xbufs[k]])

        def ln_stats(t):
            i = ln_rot.next()
            st, stb = stats[i], statb[i]
            for k in range(4):
                op(DVE, lambda k=k: nc.vector.bn_stats(out=st[:, k, :], in_=X[:, t, k * 512:(k + 1) * 512]),
                   reads=[Xb[t][k]], writes=[stb[k]])
            op(DVE, lambda: nc.vector.bn_aggr(out=mvall[:, t, 0:2], in_=st[:, 0:4, :].rearrange("p k s -> p (k s)")),
               reads=stb[0:4], writes=[mvallb[t]])

        def ln_rstd(tiles):
            t0, t1 = tiles[0], tiles[-1] + 1
            act(mvall[:, t0:t1, 2], mvall[:, t0:t1, 1], AF.Sqrt, reads=mvallb[t0:t1] + [constb], writes=mvallb[t0:t1],
                bias=epst[:, 0:1], scale=1.0)
            op(DVE, lambda: nc.vector.reciprocal(out=mvall[:, t0:t1, 2], in_=mvall[:, t0:t1, 2]),
               reads=mvallb[t0:t1], writes=mvallb[t0:t1])

        def ln_apply(t):
            for k in range(4):
                sl = slice(k * 512, (k + 1) * 512)
                stt(X[:, t, sl], X[:, t, sl], mvall[:, t, 0:1], lng[:, sl], ALU.subtract, ALU.mult,
                    [Xb[t][k], mvallb[t], lngb], [Xb[t][k]])
            for k in range(4):
                sl = slice(k * 512, (k + 1) * 512)
                stt(X[:, t, sl], X[:, t, sl], mvall[:, t, 2:3], lnb[:, sl], ALU.mult, ALU.add,
                    [Xb[t][k], mvallb[t], lnbb], [Xb[t][k]])

        def fm_cast(t):
            act(xb16[t % 2][:, :], X[:, t, :], AF.Copy, reads=Xb[t], writes=[xb16b[t % 2]])

        def fm_tr(t):
            xb, xbb = xb16[t % 2], xb16b[t % 2]
            for half in range(2):
                pt, pb = ptp.next()
                for k in range(8):
                    dc = half * 8 + k
                    op(PE, lambda k=k, dc=dc, pt=pt: nc.tensor.transpose(pt[:, k, :], xb[:, dc * 128:(dc + 1) * 128], identb[:]),
                       reads=[xbb, identbb], writes=[pb], acc=(k > 0))
                act(actT[:, half * 8:(half + 1) * 8, t * 128:(t + 1) * 128], pt[:, :, :], AF.Copy, [pb], [actb[t]])

        def to_featmajor(t):
            fm_cast(t)
            fm_tr(t)

        def ln_tail(tiles, pre_done, do_fm, ff1b0=None):
            n = len(tiles)
            for i in range(n + 2):
                if i < n and i >= pre_done:
                    ln_apply(tiles[i])
                    if do_fm:
                        fm_cast(tiles[i])
                if do_fm and 1 <= i <= n:
                    fm_tr(tiles[i - 1])
                if ff1b0 is not None and 2 <= i <= n + 1:
                    ff1b0(tiles[i - 2])

        def ln_inline(tiles, i, do_fm):
            t = tiles[i]
            ln_stats(t)
            n = len(tiles)
            if n >= 4 and i == 2:
                ln_rstd(tiles[0:3])
                ln_apply(tiles[0])
                if do_fm:
                    fm_cast(tiles[0])
            if i == n - 1:
                ln_rstd(tiles[3:] if n >= 4 else tiles)
            return 1 if n >= 4 else 0

        def load_bcast(dst, src_row, buf, ds):
            dma(SP, dst, src_row.partition_broadcast(128), ds, writes=[buf])

        for dst, src in ((identf, ident_d), (tril, tril_d), (blk, blk_d), (flag, flag_d), (rc0, rc0_d)):
            dma(SP, dst[:], src, cst_ds)
        constb.w = ('d', cst_ds, cst_ds.count)
        vmemset(epst[:], EPS, [constb])
        dma(POOL, sel[:], sel_d, sel_ds, writes=[ones1b])
        act(identb[:], identf[:], AF.Copy, reads=[constb], writes=[identbb])

        def load_params(l, ps_):
            dma(POOL, wst[:], wsT_d[l].rearrange("h j i -> j h i"), wst_ds, writes=[wstb])
            for h in range(8):
                tt(WT[:, h, :], wst[:, h, :], tril[:], ALU.mult, [wstb, constb], [WTb])
            if ps_ == 1:
                dma(POOL, wst[:], wsTp_d[l].rearrange("h j i -> j h i"), wst_ds, writes=[wstb])
                for h in range(8):
                    tt(WTs[:, h, :], wst[:, h, :], blk[:], ALU.mult, [wstb, constb], [WTsb])
            dma(SP, bsf[:, 0:128], bs_d[l], bs_ds, writes=[bsfb])
            dma(SP, bsf[:, 128:256], bss_d[l], bs_ds)
            bsfb.w = ('d', bs_ds, bs_ds.count)
            vcopy(bsh[:], bsf[:], [bsfb], [bshb])
            tt(bsl[:], bsf[:], bsh[:], ALU.subtract, [bsfb, bshb], [bshb])
            dma(POOL, wpool[:], w_pool[l].rearrange("g c e -> c g e"), wpoold, writes=[wpoolb])
            dma(SP, pscale[:], pscale_d[l], small_ds, writes=[smallb])
            dma(SP, wconv[:], wconv_d[l], small_ds)
            smallb.w = ('d', small_ds, small_ds.count)

        def phase_v(l, ps_, lo):
            switch(gvb + [vgb, vbb])
            load_bcast(vg, lnv_g[l:l + 1, :], vgb, ln_ds)
            load_bcast(vbeta, lnv_b[l:l + 1, :], vbb, ln_ds)
            vgb.w = vbb.w = ('d', ln_ds, ln_ds.count)
            s0, b0 = w_get()
            s1, b1 = w_get(hold=1)
            slots = ((s0, b0), (s1, b1))

            def v_mm(t, vbk):
                sl, sbf = slots[vbk]
                pt, pb = pp.next()
                for dc in range(16):
                    mm(pt[:, :], actT[:, dc, t * 128:(t + 1) * 128], sl[:, dc, :], dc == 0, dc == 15,
                       [actb[t], sbf], pb, dc == 0)
                return pt, pb

            def v_post(t, halves):
                g_ = gv[t % 2]
                gb_ = gvb[t % 2]
                for vbk, (pt, pb) in enumerate(halves):
                    act(g_[:, vbk * 512:(vbk + 1) * 512], pt[:, :], AF.Gelu_apprx_tanh, [pb], [gb_])
                layernorm(g_, [gb_, gb_], 1024, vg, vbeta, vgb, vbb)
                if ps_ == 1 and t == NT - 1:
                    dma(SP, chunkv_d[l], g_, gv_ds, reads=[gb_])
                act(vtm[:, t, :], g_, AF.Copy, [gb_], [vtb[t]])

            vt = list(range(lo, NT))
            for t in vt[:-2]:
                v_post(t, [v_mm(t, 0), v_mm(t, 1)])
            ta, tb = vt[-2], vt[-1]
            a0 = v_mm(ta, 0)
            b0_ = v_mm(tb, 0)
            a1 = v_mm(ta, 1)
            b1_ = v_mm(tb, 1)
            v_post(ta, [a0, a1])
            v_post(tb, [b0_, b1_])

        def phase_a(l, ps_, tlo, tloh):
            switch([peb, sab, sbbb])
            sl, sbf = w_get()
            if ps_ == 1:
                for e in range(4):
                    pt, pb = pp.next()
                    op(PE, lambda pt=pt, e=e: nc.tensor.transpose(pt[:, 0:32], stc[0:32, e * 128:(e + 1) * 128], identf[0:32, 0:32]),
                       reads=[stpb, constb], writes=[pb])
                    vcopy(czs[:, e, :, 0:2], pt[:, 0:32].rearrange("p (s r) -> p s r", r=2), [pb], [czsb])
            for g in range(4):
                w = WINDOWS[g]
                for (c0, c1) in cgroups(tloh * 128):
                    pt, pb = pp.next()
                    for dc in range(16):
                        mm(pt[:, 0:c1 - c0], sl[:, dc, g * 128:(g + 1) * 128], actT[:, dc, c0:c1],
                           dc == 0, dc == 15, [sbf] + actbufs(c0, c1), pb, dc == 0)
                    act(pe_v[:, 15 + c0:15 + c1], pt[:, 0:c1 - c0], AF.Copy, [pb], [peb])
                hb = hsb[l][g]
                if ps_ == 0:
                    vmemset(pe_v[:, 0:15], 0.0, [peb])
                    ts(pe_v[:, 384:399], pe_v[:, 384:399], flag[:, 0:1], None, ALU.mult, None, [peb, constb], [peb])
                    vcopy(hsave_p[:, l * 4 + g, :], pe_v[:, 768:783], [peb], [hb])
                else:
                    vcopy(pe_v[:, 0:15], hsave_p[:, l * 4 + g, :], [hb], [peb])
                    for j, c0 in enumerate((512, 640)):
                        pt, pb = pp.next()
                        op(PE, lambda c0=c0, pt=pt: nc.tensor.transpose(pt[:, 0:128], pe_v[:, 15 + c0:15 + c0 + 128], identf[:]),
                           reads=[peb, constb], writes=[pb])
                        vcopy(tmst[j][:, g * 128:(g + 1) * 128], pt[:, 0:128], [pb], [tmstb[j]])
                cur, curb = pe_v, peb
                nxts = [(sa_v, sab), (sbb_v, sbbb)]
                for k in range(g + 1):
                    sh = 1 << k
                    lo = (1 << (k + 1)) - 1
                    nx, nxb = nxts[k % 2]
                    tt(nx[:, lo:783], cur[:, lo:783], cur[:, lo - sh:783 - sh], ALU.add, [curb], [nxb])
                    cur, curb = nx, nxb
                stt(dd[:, 0:T], cur[:, 15:783], 1.0 / w, pe_v[:, 15:783], ALU.mult, ALU.subtract, [curb, peb], [ddb])
                if ps_ == 0:
                    tt(tmp15[:, 0:15], cur[:, 399:414], rc0[:, g * 16:g * 16 + 15], ALU.mult, [curb, constb], [tmp15b])
                    tt(dd[:, 384:399], tmp15[:, 0:15], pe_v[:, 399:414], ALU.subtract, [tmp15b, peb], [ddb])
                else:
                    p0, p0b = pes[0], pesb[0]
                    for half in range(2):
                        pt, pb = pp.next()
                        op(PE, lambda half=half, pt=pt: nc.tensor.transpose(pt[:, 0:120], stp[0:120, half, g * 128:(g + 1) * 128],
                                                                           identf[0:120, 0:120]),
                           reads=[stpb, constb], writes=[pb])
                        vcopy(p0[:, half * 8:(half + 1) * 8, 0:15], pt[:, 0:120].rearrange("p (s r) -> p s r", r=15), [pb], [p0b])
                    vcopy(p0[:, :, 15:23], pe_v[:, 655:783].rearrange("p (s t) -> p s t", t=8), [peb], [p0b])
                    c_, cb_ = p0, p0b
                    nx2 = [(pes[1], pesb[1]), (pes[2], pesb[2])]
                    for k in range(g + 1):
                        sh = 1 << k
                        lo = (1 << (k + 1)) - 1
                        nx, nxb = nx2[k % 2]
                        tt(nx[:, :, lo:23], c_[:, :, lo:23], c_[:, :, lo - sh:23 - sh], ALU.add, [cb_], [nxb])
                        c_, cb_ = nx, nxb
                    stt(dd[:, 640:768].rearrange("p (s t) -> p s t", t=8), c_[:, :, 15:23], 1.0 / w, p0[:, :, 15:23],
                        ALU.mult, ALU.subtract, [cb_, p0b], [ddb])
                for (c0, c1) in cgroups(tlo * 128):
                    pt, pb = pp.next()
                    mm(pt[:, 0:c1 - c0], wpool[:, g, :], dd[:, c0:c1], True, True, [wpoolb, ddb], pb, True)
                    act(mixT[:, g, c0:c1], pt[:, 0:c1 - c0], AF.Copy, [pb, smallb],
                        mixbufs(g, c0, c1), scale=pscale[:, g:g + 1])
            if ps_ == 1:
                dma(SP, ptm_p_d[l], tmst[0][:], tmstd[0], reads=[tmstb[0]])
                dma(SP, ptm_s_d[l], tmst[1][:], tmstd[1], reads=[tmstb[1]])

        def phase_c(l, ps_, lo, loh):
            switch(gcsb + czeb)
            sl, sbf = w_get()
            for e in range(4):
                for (c0, c1) in cgroups(loh * 128):
                    pt, pb = pp.next()
                    for dc in range(16):
                        mm(pt[:, 0:c1 - c0], sl[:, dc, e * 128:(e + 1) * 128], actT[:, dc, c0:c1],
                           dc == 0, dc == 15, [sbf] + actbufs(c0, c1), pb, dc == 0)
                    act(gcs[e][:, c0:c1], pt[:, 0:c1 - c0], AF.Copy, [pb], [gcsb[e]])
            sl, sbf = w_get()
            for e in range(4):
                for (c0, c1) in cgroups(loh * 128):
                    pt, pb = pp.next()
                    for dc in range(16):
                        mm(pt[:, 0:c1 - c0], sl[:, dc, e * 128:(e + 1) * 128], actT[:, dc, c0:c1],
                           dc == 0, dc == 15, [sbf] + actbufs(c0, c1), pb, dc == 0)
                    tt(cze[e][:, 2 + c0:2 + c1], pt[:, 0:c1 - c0], gcs[e][:, c0:c1], ALU.mult,
                       [pb, gcsb[e]], [czeb[e]])
                hb = hsb[l][4 + e]
                if ps_ == 0:
                    vmemset(cze[e][:, 0:2], 0.0, [czeb[e]])
                    ts(cze[e][:, 384:386], cze[e][:, 384:386], flag[:, 0:1], None, ALU.mult, None, [czeb[e], constb], [czeb[e]])
                    vcopy(hsave_c[:, l * 4 + e, :], cze[e][:, 768:770], [czeb[e]], [hb])
                else:
                    vcopy(cze[e][:, 0:2], hsave_c[:, l * 4 + e, :], [hb], [czeb[e]])
                    for j, c0 in enumerate((512, 640)):
                        pt, pb = pp.next()
                        op(PE, lambda c0=c0, pt=pt, e=e: nc.tensor.transpose(pt[:, 0:128], cze[e][:, 2 + c0:2 + c0 + 128], identf[:]),
                           reads=[czeb[e], constb], writes=[pb])
                        vcopy(tmst[j][:, e * 128:(e + 1) * 128], pt[:, 0:128], [pb], [tmstb[j]])
                y = gcs[e]
                ts(y[:, 0:T], cze[e][:, 0:T], wconv[:, e:e + 1], None, ALU.mult, None, [czeb[e], smallb], [gcsb[e]])
                stt(y[:, 0:T], cze[e][:, 1:T + 1], wconv[:, 4 + e:5 + e], y[:, 0:T], ALU.mult, ALU.add,
                    [czeb[e], smallb, gcsb[e]], [gcsb[e]])
                stt(y[:, 0:T], cze[e][:, 2:T + 2], wconv[:, 8 + e:9 + e], y[:, 0:T], ALU.mult, ALU.add,
                    [czeb[e], smallb, gcsb[e]], [gcsb[e]])
                if ps_ == 1:
                    vcopy(czs[:, e, :, 2:10], cze[e][:, 642:770].rearrange("p (s t) -> p s t", t=8), [czeb[e]], [czsb])
                    ts(ysm[:, :, :], czs[:, e, :, 0:8], wconv[:, e:e + 1], None, ALU.mult, None, [czsb, smallb], [ysmb])
                    stt(ysm[:, :, :], czs[:, e, :, 1:9], wconv[:, 4 + e:5 + e], ysm[:, :, :], ALU.mult, ALU.add,
                        [czsb, smallb, ysmb], [ysmb])
                    stt(y[:, 640:768].rearrange("p (s t) -> p s t", t=8), czs[:, e, :, 2:10], wconv[:, 8 + e:9 + e], ysm[:, :, :],
                        ALU.mult, ALU.add, [czsb, smallb, ysmb, gcsb[e]], [gcsb[e]])
            if ps_ == 1:
                dma(SP, cztm_p_d[l], tmst[0][:], tmstd[0], reads=[tmstb[0]])
                dma(SP, cztm_s_d[l], tmst[1][:], tmstd[1], reads=[tmstb[1]])
            sl, sbf = w_get()
            for e in range(4):
                for (c0, c1) in cgroups(lo * 128):
                    pt, pb = pp.next()
                    for dc in range(16):
                        mm(pt[:, 0:c1 - c0], sl[:, dc, e * 128:(e + 1) * 128], actT[:, dc, c0:c1],
                           dc == 0, dc == 15, [sbf] + actbufs(c0, c1), pb, dc == 0)
                    tt(mixT[:, 12 + e, c0:c1], pt[:, 0:c1 - c0], gcs[e][:, c0:c1], ALU.mult,
                       [pb, gcsb[e]], mixbufs(12 + e, c0, c1))

        def phase_b(l, ps_, lo):
            if ps_ == 1:
                switch([stpb])
                dma(SP, stp, st_pool[l].rearrange("(h r) c -> r h c", r=120), st_ds, writes=[stpb])
                dma(SP, stc, st_conv[l], st_ds)
                stpb.w = ('d', st_ds, st_ds.count)
                dma(SP, hist_s_d[l], st_pool[l].rearrange("(s r) c -> s r c", r=15)[:, 8:15, :], hist_ds)
            cur = {}

            def u_part(h):
                e = h % 4
                if e == 0:
                    cur['sl'], cur['sbf'] = w_get()
                sl, sbf = cur['sl'], cur['sbf']
                u_, ub_ = ug[h % 2], ugb[h % 2]
                for (c0, c1) in cgroups(lo * 128):
                    pt, pb = pp.next()
                    for dc in range(16):
                        mm(pt[:, 0:c1 - c0], sl[:, dc, e * 128:(e + 1) * 128], actT[:, dc, c0:c1],
                           dc == 0, dc == 15, [sbf] + actbufs(c0, c1), pb, dc == 0)
                    act(u_[:, c0:c1], pt[:, 0:c1 - c0], AF.Gelu_apprx_tanh, [pb], [ub_])

            def b_part(h):
                u_, ub_ = ug[h % 2], ugb[h % 2]
                for (t0, t1) in ((lo, min(lo + 4, NT)), (min(lo + 4, NT), NT)):
                    if t0 >= t1:
                        continue
                    pt, pb = pp.next()
                    first = True
                    for t in range(t0, t1):
                        samp = (ps_ == 1 and t == NT - 1)
                        wt_, wtb_ = (WTs, WTsb) if samp else (WT, WTb)
                        bo = 128 if samp else 0
                        o = pt[:, (t - t0) * 128:(t - t0 + 1) * 128]
                        mm(o, vtm[:, t, h * 128:(h + 1) * 128], wt_[:, h, :], True, False, [vtb[t], wtb_], pb, first)
                        first = False
                        mm(o, sel[0:8, h * 128:(h + 1) * 128], bsh[0:8, bo:bo + 128], False, False, [ones1b, bshb], pb, False)
                        mm(o, sel[0:8, h * 128:(h + 1) * 128], bsl[0:8, bo:bo + 128], False, True, [ones1b, bshb], pb, False)
                    c0, c1 = t0 * 128, t1 * 128
                    tt(mixT[:, 4 + h, c0:c1], pt[:, 0:c1 - c0], u_[:, c0:c1], ALU.mult, [pb, ub_], mixbufs(4 + h, c0, c1))

            for h in range(8):
                u_part(h)
                if h >= 1:
                    b_part(h - 1)
            b_part(7)

        def phase_out(l, ps_, tiles):
            switch([lngb, lnbb])
            load_bcast(lng, ln1_g[l:l + 1, :], lngb, ln_ds)
            load_bcast(lnb, ln1_b[l:l + 1, :], lnbb, ln_ds)
            lngb.w = lnbb.w = ('d', ln_ds, ln_ds.count)
            for cb in range(4):
                sl, sbf = w_get()
                for t in tiles:
                    pt, pb = pp.next()
                    for ec in range(16):
                        mm(pt[:, :], mixT[:, ec, t * 128:(t + 1) * 128], sl[:, ec, :], ec == 0, ec == 15,
                           [mixb[ec][t], sbf], pb, ec == 0)
                    xs = X[:, t, cb * 512:(cb + 1) * 512]
                    stt(xs, xs, ALPHA, pt[:, :], ALU.mult, ALU.add, [pb, Xb[t][cb]], [Xb[t][cb]])
                    if cb == 3:
                        pre = ln_inline(tiles, tiles.index(t), True)
            sl, sbf = w_get()

            def ff1b0(t):
                for e in range(4):
                    pt, pb = pp.next()
                    for dc in range(16):
                        mm(pt[:, 0:128], sl[:, dc, e * 128:(e + 1) * 128], actT[:, dc, t * 128:(t + 1) * 128],
                           dc == 0, dc == 15, [sbf, actb[t]], pb, dc == 0)
                    relu2(mixT[:, e, t * 128:(t + 1) * 128], pt[:, 0:128], 128, pb, [mixb[e][t]])

            ln_tail(tiles, pre, True, ff1b0)
            load_bcast(lng, ln2_g[l:l + 1, :], lngb, ln_ds)
            load_bcast(lnb, ln2_b[l:l + 1, :], lnbb, ln_ds)
            lngb.w = lnbb.w = ('d', ln_ds, ln_ds.count)

        def phase_ffn(l, ps_, tiles, last):
            c_lo = tiles[0] * 128
            ncol = T - c_lo
            ngrp = 2 if ncol > 512 else 1
            gw = ncol // ngrp
            for s in range(4):
                for fb in range(4):
                    if s == 0 and fb == 0:
                        continue
                    sl, sbf = w_get()
                    for e in range(4):
                        fc = fb * 4 + e
                        for gi in range(ngrp):
                            c0 = c_lo + gi * gw
                            c1 = c0 + gw
                            pt, pb = pp.next()
                            for dc in range(16):
                                mm(pt[:, 0:gw], sl[:, dc, e * 128:(e + 1) * 128], actT[:, dc, c0:c1],
                                   dc == 0, dc == 15, [sbf] + actbufs(c0, c1), pb, dc == 0)
                            relu2(mixT[:, fc, c0:c1], pt[:, 0:gw], gw, pb, mixbufs(fc, c0, c1))
                for db in range(4):
                    sl, sbf = w_get()
                    for t in tiles:
                        pt, pb = pp.next()
                        for fc in range(16):
                            mm(pt[:, :], mixT[:, fc, t * 128:(t + 1) * 128], sl[:, fc, :], fc == 0, fc == 15,
                               [mixb[fc][t], sbf], pb, fc == 0)
                        xs = X[:, t, db * 512:(db + 1) * 512]
                        if s == 0:
                            stt(xs, xs, ALPHA, pt[:, :], ALU.mult, ALU.add, [pb, Xb[t][db]], [Xb[t][db]])
                        else:
                            tt(xs, pt[:, :], xs, ALU.add, [pb, Xb[t][db]], [Xb[t][db]])
                        if s == 3 and db == 3:
                            pre = ln_inline(tiles, tiles.index(t), not last)
            ln_tail(tiles, pre, not last)

        out_tok = []
        for ps_ in range(npass):
            for t in range(NT):
                dma(SP, X[:, t, :], x_in[ps_ * NT + t], Xds[t], writes=Xb[t])
            for t in range(NT):
                to_featmajor(t)
            for l in range(depth):
                load_params(l, ps_)
                lo = min(l, 3) if (ps_ == 0 and npass == 2) else 0
                loh = max(lo - 1, 0)
                phase_v(l, ps_, lo)
                phase_b(l, ps_, lo)
                phase_a(l, ps_, lo, loh)
                phase_c(l, ps_, lo, loh)
                tiles = list(range(min(l, 3), NT)) if (ps_ == 0 and npass == 2) else list(range(NT))
                phase_out(l, ps_, tiles)
                phase_ffn(l, ps_, tiles, l == depth - 1)
            ytiles = (3, 4, 5) if ps_ == 0 else tuple(range(NT))
            ybase = 0 if ps_ == 0 else 3
            if npass == 1:
                ytiles, ybase = (3, 4, 5), 0
            for i, t in enumerate(ytiles):
                if ps_ == 0:
                    yi = i
                else:
                    yi = ybase + i
                dma(SP, y_d[yi], X[:, t, :], Xds[t], reads=Xb[t])

        for ds in Xds + tmstd + [gv_ds, hist_ds]:
            if ds.count:
                SP.wait_for(('d', ds, ds.count))
    return nc


def make_in_maps(inp, depth=DEPTH):
    f32 = np.float32
    xp, xs = inp['x_prompt'], inp['x_sample']
    ident = np.eye(128, dtype=f32)
    jj, ii = np.meshgrid(np.arange(128), np.arange(128), indexing='ij')
    tril = (jj <= ii).astype(f32)
    blk = ((jj // 8 == ii // 8) & (jj <= ii)).astype(f32)
    w_s = inp['w_s'][:depth]
    wsT = np.ascontiguousarray(w_s.transpose(0, 1, 3, 2))
    wsTp = np.ascontiguousarray(np.tile(w_s[:, :, :8, :8].transpose(0, 1, 3, 2), (1, 1, 16, 16)))
    bs = np.ascontiguousarray(inp['b_s'][:depth])
    bss = np.ascontiguousarray(np.tile(inp['b_s'][:depth, :, :8], (1, 1, 16)))
    sel = np.zeros((8, 8, 128), f32)
    for h in range(8):
        sel[h, h, :] = 1.0
    sel = sel.reshape(8, 1024)
    pscale = np.ascontiguousarray(inp['pool_scale'][:depth].reshape(depth, 4, 128).transpose(0, 2, 1))
    wconv = np.ascontiguousarray(inp['w_conv'][:depth].reshape(depth, 3, 4, 128).transpose(0, 3, 1, 2).reshape(depth, 128, 12))
    shared = {
        'w_in': inp['w_in'][:depth], 'w_out': inp['w_out'][:depth], 'w_ff1': inp['w_ff1'][:depth], 'w_ff2': inp['w_ff2'][:depth],
        'w_pool': inp['w_pool'][:depth], 'pscale': pscale, 'wconv': wconv,
        'lnv_g': inp['ln_v_g'][:depth], 'lnv_b': inp['ln_v_b'][:depth], 'wsT': wsT, 'wsTp': wsTp, 'bs': bs, 'bss': bss,
        'ln1_g': inp['ln1_g'][:depth], 'ln1_b': inp['ln1_b'][:depth], 'ln2_g': inp['ln2_g'][:depth], 'ln2_b': inp['ln2_b'][:depth],
        'ident': ident, 'tril': tril, 'blk': blk, 'sel': sel,
    }
    shared = {k: np.ascontiguousarray(v, dtype=f32) for k, v in shared.items()}
    maps = []
    for c in range(8):
        seq, half = c // 2, c % 2
        xt = np.zeros((12, 128, D), f32)
        if half == 1:
            xt[0:3] = xp[seq, 640:1024].reshape(3, 128, D)
        xt[3:11] = xp[seq, half * 1024:(half + 1) * 1024].reshape(8, 128, D)
        xt[11] = xs[c * 16:(c + 1) * 16].reshape(128, D)
        rc = np.zeros((4, 16), f32)
        for g, w in enumerate(WINDOWS):
            for j in range(16):
                rc[g, j] = 1.0 / (min(w, j + 1) if half == 0 else w)
        m = dict(shared)
        m['x_in'] = xt
        m['flag'] = np.full((128, 1), float(half), f32)
        m['rc0'] = np.ascontiguousarray(np.broadcast_to(rc.reshape(1, 64), (128, 64)))
        m['st_pool'] = np.ascontiguousarray(inp['state_pool'][:depth, c * 16:(c + 1) * 16].reshape(depth, 240, 512))
        m['st_conv'] = np.ascontiguousarray(inp['state_conv'][:depth, c * 16:(c + 1) * 16].reshape(depth, 32, 512))
        maps.append(m)
    return maps


def assemble(results, depth=DEPTH):
    f32 = np.float32
    y_prompt = np.zeros((4, 2048, D), f32)
    y_sample = np.zeros((128, 8, D), f32)
    pool_p = np.zeros((depth, 4, 15, 512), f32)
    conv_p = np.zeros((depth, 4, 2, 512), f32)
    pool_s = np.zeros((depth, 128, 15, 512), f32)
    conv_s = np.zeros((depth, 128, 2, 512), f32)
    chunk_v = np.zeros((depth, 128, 8, 1024), f32)
    for c, r in enumerate(results):
        seq, half = c // 2, c % 2
        y_prompt[seq, half * 1024:(half + 1) * 1024] = r['y'][0:8].reshape(1024, D)
        y_sample[c * 16:(c + 1) * 16] = r['y'][8].reshape(16, 8, D)
        sl = slice(c * 16, (c + 1) * 16)
        if half == 1:
            pool_p[:, seq] = r['ptm_p'][:, 113:128]
            conv_p[:, seq] = r['cztm_p'][:, 126:128]
        pool_s[:, sl, 0:7] = r['hist_s']
        pool_s[:, sl, 7:15] = r['ptm_s'].reshape(depth, 16, 8, 512)
        conv_s[:, sl] = r['cztm_s'].reshape(depth, 16, 8, 512)[:, :, 6:8]
        chunk_v[:, sl] = r['chunkv'].reshape(depth, 16, 8, 1024)
    return (y_prompt, y_sample, pool_p, conv_p, pool_s, conv_s, chunk_v)


def kernel(**inputs):
    inp = {k: np.asarray(v) for k, v in inputs.items()}
    nc = build(DEPTH, 2)
    maps = make_in_maps(inp, DEPTH)
    res = run_bass_kernel_spmd(nc, maps, core_ids=list(range(8)))
    return assemble(res.results, DEPTH)
```

```python
import bisect
from contextlib import ExitStack

import numpy as np
import concourse.bass as bass
import concourse.mybir as mybir
from concourse.bass_utils import run_bass_kernel_spmd

F32 = mybir.dt.float32
BF16 = mybir.dt.bfloat16
AF = mybir.ActivationFunctionType
ALU = mybir.AluOpType

DEPTH = 4
D = 2048
NT = 6
T = NT * 128
TG = 384
ALPHA = (2 * DEPTH) ** 0.25
EPS = 1e-5
WINDOWS = (2, 4, 8, 16)
NSLOT = 2


class Eng:
    def __init__(self, h, sem, name):
        self.h, self.sem, self.name = h, sem, name
        self.instrs = []
        self.inc_idx = []
        self.inc_val = {}
        self.n_inc = 0
        self.waited = {}
        self.eager = False

    def ensure_inc(self, idx):
        k = bisect.bisect_left(self.inc_idx, idx)
        if k < len(self.inc_idx):
            return self.inc_val[self.inc_idx[k]]
        self.n_inc += 1
        self.instrs[idx].then_inc(self.sem, 1)
        self.inc_idx.append(idx)
        self.inc_val[idx] = self.n_inc
        return self.n_inc

    def wait_for(self, dep):
        if dep[0] == 'e':
            eng, idx = dep[1], dep[2]
            sem, val, key = eng.sem, eng.ensure_inc(idx), eng.name
        else:
            ds, val = dep[1], dep[2]
            sem, key = ds.sem, ds.name
        if self.waited.get(key, 0) >= val:
            return
        self.h.wait_ge(sem, val)
        self.waited[key] = val


class DSem:
    def __init__(self, sem, name):
        self.sem, self.name, self.count = sem, name, 0


class Buf:
    __slots__ = ('name', 'w', 'r')

    def __init__(self, name=''):
        self.name, self.w, self.r = name, None, {}


def _deps(reads, writes, acc):
    deps = []
    for b in reads:
        if b.w is not None:
            deps.append(b.w)
    if not acc:
        for b in writes:
            if b.r:
                deps.extend(b.r.values())
            elif b.w is not None:
                deps.append(b.w)
    return deps


def op(eng, fn, reads=(), writes=(), acc=False, force_inc=False):
    for d in _deps(reads, writes, acc):
        eng.wait_for(d)
    ins = fn()
    idx = len(eng.instrs)
    eng.instrs.append(ins)
    if eng.eager or force_inc:
        eng.ensure_inc(idx)
    tok = ('e', eng, idx)
    for b in reads:
        b.r[eng.name] = tok
    for b in writes:
        b.w = tok
        if not acc:
            b.r = {}
    return tok


def dma(q, out, in_, ds, reads=(), writes=()):
    for d in _deps(reads, writes, False):
        q.wait_for(d)
    q.h.dma_start(out=out, in_=in_).then_inc(ds.sem, 16)
    ds.count += 16
    tok = ('d', ds, ds.count)
    for b in reads:
        b.r[ds.name] = tok
    for b in writes:
        b.w = tok
        b.r = {}
    return tok


def _tok_key(tok):
    return tok[1].name


def _tok_later(a, b):
    return a if a[2] >= b[2] else b


def inherit(news, olds):
    merged = {}
    for o in olds:
        toks = list(o.r.values())
        if o.w is not None:
            toks.append(o.w)
        for t in toks:
            k = _tok_key(t)
            merged[k] = _tok_later(merged[k], t) if k in merged else t
    for n in news:
        n.r = dict(merged)
        n.w = None


class Rot:
    def __init__(self, items):
        self.items, self.i = items, 0

    def next(self):
        it = self.items[self.i % len(self.items)]
        self.i += 1
        return it


def build(depth=DEPTH, npass=2):
    nc = bass.Bass("TRN2", target_bir_lowering=False)

    def din(name, shape):
        return nc.dram_tensor(name, list(shape), F32, kind="ExternalInput").ap()

    def dout(name, shape):
        return nc.dram_tensor(name, list(shape), F32, kind="ExternalOutput").ap()

    x_in = din("x_in", [12, 128, D])
    flag_d = din("flag", [128, 1])
    rc0_d = din("rc0", [128, 64])
    st_pool = din("st_pool", [depth, 240, 512])
    st_conv = din("st_conv", [depth, 32, 512])
    w_in = din("w_in", [depth, D, 4096])
    w_out = din("w_out", [depth, D, D])
    w_ff1 = din("w_ff1", [depth, D, 8192])
    w_ff2 = din("w_ff2", [depth, 8192, D])
    w_pool = din("w_pool", [depth, 4, 128, 128])
    pscale_d = din("pscale", [depth, 128, 4])
    wconv_d = din("wconv", [depth, 128, 12])
    lnv_g = din("lnv_g", [depth, 1024])
    lnv_b = din("lnv_b", [depth, 1024])
    wsT_d = din("wsT", [depth, 8, 128, 128])
    wsTp_d = din("wsTp", [depth, 8, 128, 128])
    bs_d = din("bs", [depth, 8, 128])
    bss_d = din("bss", [depth, 8, 128])
    sel_d = din("sel", [8, 1024])
    ln1_g = din("ln1_g", [depth, D])
    ln1_b = din("ln1_b", [depth, D])
    ln2_g = din("ln2_g", [depth, D])
    ln2_b = din("ln2_b", [depth, D])
    ident_d = din("ident", [128, 128])
    tril_d = din("tril", [128, 128])
    blk_d = din("blk", [128, 128])

    y_d = dout("y", [9, 128, D])
    ptm_p_d = dout("ptm_p", [depth, 128, 512])
    cztm_p_d = dout("cztm_p", [depth, 128, 512])
    ptm_s_d = dout("ptm_s", [depth, 128, 512])
    cztm_s_d = dout("cztm_s", [depth, 128, 512])
    hist_s_d = dout("hist_s", [depth, 16, 7, 512])
    chunkv_d = dout("chunkv", [depth, 128, 1024])

    with ExitStack() as es:
        def sb(name, shape, dt=F32):
            return es.enter_context(nc.sbuf_tensor(name, list(shape), dt))

        def ps(name, shape, dt=F32):
            return es.enter_context(nc.psum_tensor(name, list(shape), dt))

        def sem(name):
            return es.enter_context(nc.semaphore(name))

        PE = Eng(nc.tensor, sem("s_pe"), "pe")
        ACT = Eng(nc.scalar, sem("s_act"), "act")
        DVE = Eng(nc.vector, sem("s_dve"), "dve")
        POOL = Eng(nc.gpsimd, sem("s_pool"), "pool")
        SP = Eng(nc.sync, sem("s_sp"), "sp")
        ACT.eager = True
        DVE.eager = True

        def dsem(name):
            return DSem(sem(name), name)

        X = sb("X", [128, NT, D])
        Xb = [[Buf() for _ in range(4)] for _ in range(NT)]
        Xds = [dsem(f"xds{t}") for t in range(NT)]
        actT = sb("actT", [128, 16, T], BF16)
        actb = [Buf() for _ in range(NT)]
        mixT = sb("mixT", [128, 16, T], BF16)
        mixb = [[Buf() for _ in range(NT)] for _ in range(16)]
        vtm = sb("vtm", [128, NT, 1024], BF16)
        vtb = [Buf() for _ in range(NT)]
        wsl = [sb(f"wsl{i}", [128, 16, 512], BF16) for i in range(NSLOT)]
        wslb = [Buf() for _ in range(NSLOT)]
        wsld = [dsem(f"wsld{i}") for i in range(NSLOT)]
        R = sb("R", [128, 6160])
        xb16 = [sb(f"xb16_{i}", [128, D], BF16) for i in range(2)]
        xb16b = [Buf(), Buf()]
        dd = sb("dd", [128, T], BF16)
        ddb = Buf()
        ug = [sb(f"ug{i}", [128, T], BF16) for i in range(2)]
        ugb = [Buf() for _ in range(2)]
        rl = [sb(f"rl{i}", [128, 512]) for i in range(2)]
        rlb = [Buf() for _ in range(2)]
        wst = sb("wst", [128, 8, 128], BF16)
        wstb = Buf()
        WT = sb("WT", [128, 8, 128], BF16)
        WTb = Buf()
        WTs = sb("WTs", [128, 8, 128], BF16)
        WTsb = Buf()
        bsf = sb("bsf", [8, 256])
        bsfb = Buf()
        bsh = sb("bsh", [8, 256], BF16)
        bsl = sb("bsl", [8, 256], BF16)
        bshb = Buf()
        sel = sb("sel_t", [8, 1024], BF16)
        ones1b = Buf()
        wpool = sb("wpool_t", [128, 4, 128], BF16)
        wpoolb = Buf()
        wpoold = dsem("wpoold")
        pscale = sb("pscale_t", [128, 4])
        wconv = sb("wconv_t", [128, 12])
        smallb = Buf()
        identf = sb("identf", [128, 128])
        identb = sb("identb", [128, 128], BF16)
        tril = sb("tril_t", [128, 128])
        blk = sb("blk_t", [128, 128])
        flag = sb("flagt", [128, 1])
        rc0 = sb("rc0t", [128, 64])
        epst = sb("epst", [128, 1])
        constb = Buf()
        identbb = Buf()
        hsave_p = sb("hsave_p", [128, depth * 4, 15])
        hsave_c = sb("hsave_c", [128, depth * 4, 2])
        hsb = [[Buf() for _ in range(8)] for _ in range(depth)]
        stats = [sb(f"stats{i}", [128, 4, 6]) for i in range(3)]
        statb = [[Buf() for _ in range(4)] for _ in range(3)]
        mvs = [sb(f"mv{i}", [128, 4]) for i in range(3)]
        mvb = [Buf() for _ in range(3)]
        ln_rot = Rot([0, 1, 2])
        mvall = sb("mvall", [128, NT, 4])
        mvallb = [Buf() for _ in range(NT)]
        tmst = [sb(f"tmst{i}", [128, 512]) for i in range(2)]
        tmstb = [Buf() for _ in range(2)]
        tmstd = [dsem(f"tmstd{i}") for i in range(2)]
        stpb = Buf()
        pes = [sb(f"pes{i}", [128, 16, 23]) for i in range(3)]
        pesb = [Buf() for _ in range(3)]
        czs = sb("czs", [128, 4, 16, 10])
        czsb = Buf()
        ysm = sb("ysm", [128, 16, 8])
        ysmb = Buf()
        tmp15 = sb("tmp15", [128, 16])
        tmp15b = Buf()
        wst_ds = dsem("wst_ds")
        bs_ds = dsem("bs_ds")
        st_ds = dsem("st_ds")
        small_ds = dsem("small_ds")
        hist_ds = dsem("hist_ds")
        ln_ds = dsem("ln_ds")
        cst_ds = dsem("cst_ds")
        sel_ds = dsem("sel_ds")
        gv_ds = dsem("gv_ds")

        gv = [R[:, 0:1024], R[:, 3072:4096]]
        gvb = [Buf(), Buf()]
        vg, vbeta = R[:, 1024:2048], R[:, 2048:3072]
        vgb, vbb = Buf(), Buf()
        pe_v, sa_v, sbb_v = R[:, 0:783], R[:, 800:1583], R[:, 1600:2383]
        peb, sab, sbbb = Buf(), Buf(), Buf()
        gcs = [R[:, e * 768:(e + 1) * 768] for e in range(4)]
        gcsb = [Buf() for _ in range(4)]
        cze = [R[:, 3072 + e * 770:3072 + (e + 1) * 770] for e in range(4)]
        czeb = [Buf() for _ in range(4)]
        stp = R[0:120, 2400:3424].rearrange("p (h c) -> p h c", h=2)
        stc = R[0:32, 3424:3936]
        lng, lnb = R[:, 0:2048], R[:, 2048:4096]
        lngb, lnbb = Buf(), Buf()
        allR = gvb + [vgb, vbb, peb, sab, sbbb, stpb] + gcsb + czeb + [lngb, lnbb]

        def switch(news):
            inherit(news, allR)

        pbank = [ps(f"pb{i}", [128, 512]) for i in range(6)]
        pp = Rot([(pbank[i], Buf()) for i in range(6)])
        ptbank = [ps(f"ptb{i}", [128, 8, 128], BF16) for i in range(2)]
        ptp = Rot([(ptbank[i], Buf()) for i in range(2)])

        blocks = []
        for p_ in range(npass):
            for l in range(depth):
                wi = w_in[l].rearrange("(dc p) c -> p dc c", p=128)
                for c0 in (1536, 2048, 512, 1024, 0, 3072, 3584, 2560):
                    blocks.append(wi[:, :, c0:c0 + 512])
                wo = w_out[l].rearrange("(dc p) c -> p dc c", p=128)
                for cb in range(4):
                    blocks.append(wo[:, :, cb * 512:(cb + 1) * 512])
                w1 = w_ff1[l].rearrange("(dc p) c -> p dc c", p=128)
                w2 = w_ff2[l].rearrange("(fc p) c -> p fc c", p=128)
                for s in range(4):
                    for fb in range(4):
                        c0 = s * 2048 + fb * 512
                        blocks.append(w1[:, :, c0:c0 + 512])
                    for db in range(4):
                        blocks.append(w2[:, s * 16:(s + 1) * 16, db * 512:(db + 1) * 512])
        wstate = {'issue': 0, 'use': 0}

        def w_issue_upto(k):
            while wstate['issue'] <= min(k, len(blocks) - 1):
                b = wstate['issue']
                s = b % NSLOT
                dma(POOL, wsl[s][:], blocks[b], wsld[s], writes=[wslb[s]])
                wstate['issue'] += 1

        def w_get(hold=0):
            b = wstate['use']
            w_issue_upto(b + NSLOT - 1 - hold)
            wstate['use'] += 1
            return wsl[b % NSLOT], wslb[b % NSLOT]

        def mm(out, lhsT, rhs, start, stop, reads, bank, first):
            return op(PE, lambda: nc.tensor.matmul(out, lhsT, rhs, start=start, stop=stop),
                      reads=reads, writes=[bank], acc=not first, force_inc=bool(stop))

        def act(out, in_, func, reads, writes, bias=None, scale=None):
            kw = {}
            if bias is not None:
                kw['bias'] = bias
            if scale is not None:
                kw['scale'] = scale
            return op(ACT, lambda: nc.scalar.activation(out=out, in_=in_, func=func, **kw), reads=reads, writes=writes)

        def tt(out, in0, in1, o, reads, writes):
            return op(DVE, lambda: nc.vector.tensor_tensor(out=out, in0=in0, in1=in1, op=o), reads=reads, writes=writes)

        def ts(out, in0, s1, s2, o0, o1, reads, writes):
            if s2 is None:
                return op(DVE, lambda: nc.vector.tensor_scalar(out=out, in0=in0, scalar1=s1, scalar2=None, op0=o0),
                          reads=reads, writes=writes)
            return op(DVE, lambda: nc.vector.tensor_scalar(out=out, in0=in0, scalar1=s1, scalar2=s2, op0=o0, op1=o1),
                      reads=reads, writes=writes)

        def stt(out, in0, scalar, in1, o0, o1, reads, writes):
            return op(DVE, lambda: nc.vector.scalar_tensor_tensor(out=out, in0=in0, scalar=scalar, in1=in1, op0=o0, op1=o1),
                      reads=reads, writes=writes)

        def pstt(out, in0, scalar, in1, o0, o1, reads, writes):
            return op(POOL, lambda: nc.gpsimd.scalar_tensor_tensor(out=out, in0=in0, scalar=scalar, in1=in1, op0=o0, op1=o1),
                      reads=reads, writes=writes)

        def ptt(out, in0, in1, o, reads, writes):
            return op(POOL, lambda: nc.gpsimd.tensor_tensor(out=out, in0=in0, in1=in1, op=o), reads=reads, writes=writes)

        def pts(out, in0, s1, o0, reads, writes):
            return op(POOL, lambda: nc.gpsimd.tensor_scalar(out=out, in0=in0, scalar1=s1, scalar2=None, op0=o0),
                      reads=reads, writes=writes)

        rl_rot = Rot([0, 1])

        def relu2(out, psum_ap, n, pb, wbufs):
            ri = rl_rot.next()
            act(rl[ri][:, 0:n], psum_ap, AF.Relu, [pb], [rlb[ri]])
            act(out, rl[ri][:, 0:n], AF.Square, [rlb[ri]], wbufs)

        def vcopy(out, in_, reads, writes):
            return op(DVE, lambda: nc.vector.tensor_copy(out=out, in_=in_), reads=reads, writes=writes)

        def vmemset(ap, val, writes):
            return op(DVE, lambda: nc.vector.memset(ap, val), writes=writes)

        def cgroups(c_lo):
            ncol = T - c_lo
            ngrp = 2 if ncol > 512 else 1
            gw = ncol // ngrp
            return [(c_lo + i * gw, c_lo + (i + 1) * gw) for i in range(ngrp)]

        def tiles_of(c0, c1):
            return range(c0 // 128, (c1 + 127) // 128)

        def mixbufs(e, c0, c1):
            return [mixb[e][t] for t in tiles_of(c0, c1)]

        def actbufs(c0, c1):
            return [actb[t] for t in tiles_of(c0, c1)]

        def layernorm(xap, xbufs, W, g_ap, b_ap, gbuf, bbuf):
            nk = W // 512
            i = ln_rot.next()
            st, stb, mv, mb = stats[i], statb[i], mvs[i], mvb[i]
            for k in range(nk):
                op(DVE, lambda k=k: nc.vector.bn_stats(out=st[:, k, :], in_=xap[:, k * 512:(k + 1) * 512]),
                   reads=[xbufs[k]], writes=[stb[k]])
            op(DVE, lambda: nc.vector.bn_aggr(out=mv[:, 0:2], in_=st[:, 0:nk, :].rearrange("p k s -> p (k s)")),
               reads=stb[0:nk], writes=[mb])
            act(mv[:, 2:3], mv[:, 1:2], AF.Sqrt, reads=[mb, constb], writes=[mb], bias=epst[:, 0:1], scale=1.0)
            op(DVE, lambda: nc.vector.reciprocal(out=mv[:, 2:3], in_=mv[:, 2:3]), reads=[mb], writes=[mb])
            for k in range(nk):
                sl = slice(k * 512, (k + 1) * 512)
                stt(xap[:, sl], xap[:, sl], mv[:, 0:1], g_ap[:, sl], ALU.subtract, ALU.mult, [xbufs[k], mb, gbuf], [xbufs[k]])
            for k in range(nk):
                sl = slice(k * 512, (k + 1) * 512)
                stt(xap[:, sl], xap[:, sl], mv[:, 2:3], b_ap[:, sl], ALU.mult, ALU.add, [xbufs[k], mb, bbuf], [xbufs[k]])

        def ln_stats(t):
            i = ln_rot.next()
            st, stb = stats[i], statb[i]
            for k in range(4):
                op(DVE, lambda k=k: nc.vector.bn_stats(out=st[:, k, :], in_=X[:, t, k * 512:(k + 1) * 512]),
                   reads=[Xb[t][k]], writes=[stb[k]])
            op(DVE, lambda: nc.vector.bn_aggr(out=mvall[:, t, 0:2], in_=st[:, 0:4, :].rearrange("p k s -> p (k s)")),
               reads=stb[0:4], writes=[mvallb[t]])

        def ln_rstd(tiles):
            t0, t1 = tiles[0], tiles[-1] + 1
            act(mvall[:, t0:t1, 2], mvall[:, t0:t1, 1], AF.Sqrt, reads=mvallb[t0:t1] + [constb], writes=mvallb[t0:t1],
                bias=epst[:, 0:1], scale=1.0)
            op(DVE, lambda: nc.vector.reciprocal(out=mvall[:, t0:t1, 2], in_=mvall[:, t0:t1, 2]),
               reads=mvallb[t0:t1], writes=mvallb[t0:t1])

        def ln_apply(t):
            for k in range(4):
                sl = slice(k * 512, (k + 1) * 512)
                stt(X[:, t, sl], X[:, t, sl], mvall[:, t, 0:1], lng[:, sl], ALU.subtract, ALU.mult,
                    [Xb[t][k], mvallb[t], lngb], [Xb[t][k]])
            for k in range(4):
                sl = slice(k * 512, (k + 1) * 512)
                stt(X[:, t, sl], X[:, t, sl], mvall[:, t, 2:3], lnb[:, sl], ALU.mult, ALU.add,
                    [Xb[t][k], mvallb[t], lnbb], [Xb[t][k]])

        def fm_cast(t):
            act(xb16[t % 2][:, :], X[:, t, :], AF.Copy, reads=Xb[t], writes=[xb16b[t % 2]])

        def fm_tr(t):
            xb, xbb = xb16[t % 2], xb16b[t % 2]
            for half in range(2):
                pt, pb = ptp.next()
                for k in range(8):
                    dc = half * 8 + k
                    op(PE, lambda k=k, dc=dc, pt=pt: nc.tensor.transpose(pt[:, k, :], xb[:, dc * 128:(dc + 1) * 128], identb[:]),
                       reads=[xbb, identbb], writes=[pb], acc=(k > 0), force_inc=(k == 7))
                act(actT[:, half * 8:(half + 1) * 8, t * 128:(t + 1) * 128], pt[:, :, :], AF.Copy, [pb], [actb[t]])

        def to_featmajor(t):
            fm_cast(t)
            fm_tr(t)

        def ln_tail(tiles, pre_done, do_fm, ff1b0=None):
            n = len(tiles)
            for i in range(n + 2):
                if i < n and i >= pre_done:
                    ln_apply(tiles[i])
                    if do_fm:
                        fm_cast(tiles[i])
                if do_fm and 1 <= i <= n:
                    fm_tr(tiles[i - 1])
                if ff1b0 is not None and 2 <= i <= n + 1:
                    ff1b0(tiles[i - 2])

        def ln_inline(tiles, i, do_fm):
            t = tiles[i]
            ln_stats(t)
            n = len(tiles)
            if n >= 4 and i == 2:
                ln_rstd(tiles[0:3])
                ln_apply(tiles[0])
                if do_fm:
                    fm_cast(tiles[0])
            if i == n - 1:
                ln_rstd(tiles[3:] if n >= 4 else tiles)
            return 1 if n >= 4 else 0

        def load_bcast(dst, src_row, buf, ds):
            dma(SP, dst, src_row.partition_broadcast(128), ds, writes=[buf])

        for dst, src in ((identf, ident_d), (tril, tril_d), (blk, blk_d), (flag, flag_d), (rc0, rc0_d)):
            dma(SP, dst[:], src, cst_ds)
        constb.w = ('d', cst_ds, cst_ds.count)
        vmemset(epst[:], EPS, [constb])
        dma(POOL, sel[:], sel_d, sel_ds, writes=[ones1b])
        act(identb[:], identf[:], AF.Copy, reads=[constb], writes=[identbb])

        def load_params(l, ps_):
            dma(POOL, wst[:], wsT_d[l].rearrange("h j i -> j h i"), wst_ds, writes=[wstb])
            for h in range(8):
                tt(WT[:, h, :], wst[:, h, :], tril[:], ALU.mult, [wstb, constb], [WTb])
            if ps_ == 1:
                dma(POOL, wst[:], wsTp_d[l].rearrange("h j i -> j h i"), wst_ds, writes=[wstb])
                for h in range(8):
                    tt(WTs[:, h, :], wst[:, h, :], blk[:], ALU.mult, [wstb, constb], [WTsb])
            dma(SP, bsf[:, 0:128], bs_d[l], bs_ds, writes=[bsfb])
            dma(SP, bsf[:, 128:256], bss_d[l], bs_ds)
            bsfb.w = ('d', bs_ds, bs_ds.count)
            vcopy(bsh[:], bsf[:], [bsfb], [bshb])
            tt(bsl[:], bsf[:], bsh[:], ALU.subtract, [bsfb, bshb], [bshb])
            dma(POOL, wpool[:], w_pool[l].rearrange("g c e -> c g e"), wpoold, writes=[wpoolb])
            dma(SP, pscale[:], pscale_d[l], small_ds, writes=[smallb])
            dma(SP, wconv[:], wconv_d[l], small_ds)
            smallb.w = ('d', small_ds, small_ds.count)

        def phase_v(l, ps_, lo):
            switch(gvb + [vgb, vbb])
            load_bcast(vg, lnv_g[l:l + 1, :], vgb, ln_ds)
            load_bcast(vbeta, lnv_b[l:l + 1, :], vbb, ln_ds)
            vgb.w = vbb.w = ('d', ln_ds, ln_ds.count)
            s0, b0 = w_get()
            s1, b1 = w_get(hold=1)
            slots = ((s0, b0), (s1, b1))

            def v_mm(t, vbk):
                sl, sbf = slots[vbk]
                pt, pb = pp.next()
                for dc in range(16):
                    mm(pt[:, :], actT[:, dc, t * 128:(t + 1) * 128], sl[:, dc, :], dc == 0, dc == 15,
                       [actb[t], sbf], pb, dc == 0)
                return pt, pb

            def v_post(t, halves):
                g_ = gv[t % 2]
                gb_ = gvb[t % 2]
                for vbk, (pt, pb) in enumerate(halves):
                    act(g_[:, vbk * 512:(vbk + 1) * 512], pt[:, :], AF.Gelu_apprx_tanh, [pb], [gb_])
                layernorm(g_, [gb_, gb_], 1024, vg, vbeta, vgb, vbb)
                if ps_ == 1 and t == NT - 1:
                    dma(SP, chunkv_d[l], g_, gv_ds, reads=[gb_])
                act(vtm[:, t, :], g_, AF.Copy, [gb_], [vtb[t]])

            vt = list(range(lo, NT))
            for t in vt[:-2]:
                v_post(t, [v_mm(t, 0), v_mm(t, 1)])
            ta, tb = vt[-2], vt[-1]
            a0 = v_mm(ta, 0)
            b0_ = v_mm(tb, 0)
            a1 = v_mm(ta, 1)
            b1_ = v_mm(tb, 1)
            v_post(ta, [a0, a1])
            v_post(tb, [b0_, b1_])

        def phase_a(l, ps_, tlo, tloh):
            switch([peb, sab, sbbb])
            sl, sbf = w_get()
            if ps_ == 1:
                for e in range(4):
                    pt, pb = pp.next()
                    op(PE, lambda pt=pt, e=e: nc.tensor.transpose(pt[:, 0:32], stc[0:32, e * 128:(e + 1) * 128], identf[0:32, 0:32]),
                       reads=[stpb, constb], writes=[pb])
                    vcopy(czs[:, e, :, 0:2], pt[:, 0:32].rearrange("p (s r) -> p s r", r=2), [pb], [czsb])
            for g in range(4):
                w = WINDOWS[g]
                for (c0, c1) in cgroups(tloh * 128):
                    pt, pb = pp.next()
                    for dc in range(16):
                        mm(pt[:, 0:c1 - c0], sl[:, dc, g * 128:(g + 1) * 128], actT[:, dc, c0:c1],
                           dc == 0, dc == 15, [sbf] + actbufs(c0, c1), pb, dc == 0)
                    act(pe_v[:, 15 + c0:15 + c1], pt[:, 0:c1 - c0], AF.Copy, [pb], [peb])
                hb = hsb[l][g]
                if ps_ == 0:
                    vmemset(pe_v[:, 0:15], 0.0, [peb])
                    ts(pe_v[:, 384:399], pe_v[:, 384:399], flag[:, 0:1], None, ALU.mult, None, [peb, constb], [peb])
                    vcopy(hsave_p[:, l * 4 + g, :], pe_v[:, 768:783], [peb], [hb])
                else:
                    vcopy(pe_v[:, 0:15], hsave_p[:, l * 4 + g, :], [hb], [peb])
                    for j, c0 in enumerate((512, 640)):
                        pt, pb = pp.next()
                        op(PE, lambda c0=c0, pt=pt: nc.tensor.transpose(pt[:, 0:128], pe_v[:, 15 + c0:15 + c0 + 128], identf[:]),
                           reads=[peb, constb], writes=[pb])
                        vcopy(tmst[j][:, g * 128:(g + 1) * 128], pt[:, 0:128], [pb], [tmstb[j]])
                cur, curb = pe_v, peb
                nxts = [(sa_v, sab), (sbb_v, sbbb)]
                for k in range(g + 1):
                    sh = 1 << k
                    lo = (1 << (k + 1)) - 1
                    nx, nxb = nxts[k % 2]
                    tt(nx[:, lo:783], cur[:, lo:783], cur[:, lo - sh:783 - sh], ALU.add, [curb], [nxb])
                    cur, curb = nx, nxb
                stt(dd[:, 0:T], cur[:, 15:783], 1.0 / w, pe_v[:, 15:783], ALU.mult, ALU.subtract, [curb, peb], [ddb])
                if ps_ == 0:
                    tt(tmp15[:, 0:15], cur[:, 399:414], rc0[:, g * 16:g * 16 + 15], ALU.mult, [curb, constb], [tmp15b])
                    tt(dd[:, 384:399], tmp15[:, 0:15], pe_v[:, 399:414], ALU.subtract, [tmp15b, peb], [ddb])
                else:
                    p0, p0b = pes[0], pesb[0]
                    for half in range(2):
                        pt, pb = pp.next()
                        op(PE, lambda half=half, pt=pt: nc.tensor.transpose(pt[:, 0:120], stp[0:120, half, g * 128:(g + 1) * 128],
                                                                           identf[0:120, 0:120]),
                           reads=[stpb, constb], writes=[pb])
                        vcopy(p0[:, half * 8:(half + 1) * 8, 0:15], pt[:, 0:120].rearrange("p (s r) -> p s r", r=15), [pb], [p0b])
                    vcopy(p0[:, :, 15:23], pe_v[:, 655:783].rearrange("p (s t) -> p s t", t=8), [peb], [p0b])
                    c_, cb_ = p0, p0b
                    nx2 = [(pes[1], pesb[1]), (pes[2], pesb[2])]
                    for k in range(g + 1):
                        sh = 1 << k
                        lo = (1 << (k + 1)) - 1
                        nx, nxb = nx2[k % 2]
                        tt(nx[:, :, lo:23], c_[:, :, lo:23], c_[:, :, lo - sh:23 - sh], ALU.add, [cb_], [nxb])
                        c_, cb_ = nx, nxb
                    stt(dd[:, 640:768].rearrange("p (s t) -> p s t", t=8), c_[:, :, 15:23], 1.0 / w, p0[:, :, 15:23],
                        ALU.mult, ALU.subtract, [cb_, p0b], [ddb])
                for (c0, c1) in cgroups(tlo * 128):
                    pt, pb = pp.next()
                    mm(pt[:, 0:c1 - c0], wpool[:, g, :], dd[:, c0:c1], True, True, [wpoolb, ddb], pb, True)
                    act(mixT[:, g, c0:c1], pt[:, 0:c1 - c0], AF.Copy, [pb, smallb],
                        mixbufs(g, c0, c1), scale=pscale[:, g:g + 1])
            if ps_ == 1:
                dma(SP, ptm_p_d[l], tmst[0][:], tmstd[0], reads=[tmstb[0]])
                dma(SP, ptm_s_d[l], tmst[1][:], tmstd[1], reads=[tmstb[1]])

        def phase_c(l, ps_, lo, loh):
            switch(gcsb + czeb)
            sl, sbf = w_get()
            for e in range(4):
                for (c0, c1) in cgroups(loh * 128):
                    pt, pb = pp.next()
                    for dc in range(16):
                        mm(pt[:, 0:c1 - c0], sl[:, dc, e * 128:(e + 1) * 128], actT[:, dc, c0:c1],
                           dc == 0, dc == 15, [sbf] + actbufs(c0, c1), pb, dc == 0)
                    act(gcs[e][:, c0:c1], pt[:, 0:c1 - c0], AF.Copy, [pb], [gcsb[e]])
            sl, sbf = w_get()
            for e in range(4):
                for (c0, c1) in cgroups(loh * 128):
                    pt, pb = pp.next()
                    for dc in range(16):
                        mm(pt[:, 0:c1 - c0], sl[:, dc, e * 128:(e + 1) * 128], actT[:, dc, c0:c1],
                           dc == 0, dc == 15, [sbf] + actbufs(c0, c1), pb, dc == 0)
                    tt(cze[e][:, 2 + c0:2 + c1], pt[:, 0:c1 - c0], gcs[e][:, c0:c1], ALU.mult,
                       [pb, gcsb[e]], [czeb[e]])
                hb = hsb[l][4 + e]
                if ps_ == 0:
                    vmemset(cze[e][:, 0:2], 0.0, [czeb[e]])
                    ts(cze[e][:, 384:386], cze[e][:, 384:386], flag[:, 0:1], None, ALU.mult, None, [czeb[e], constb], [czeb[e]])
                    vcopy(hsave_c[:, l * 4 + e, :], cze[e][:, 768:770], [czeb[e]], [hb])
                else:
                    vcopy(cze[e][:, 0:2], hsave_c[:, l * 4 + e, :], [hb], [czeb[e]])
                    for j, c0 in enumerate((512, 640)):
                        pt, pb = pp.next()
                        op(PE, lambda c0=c0, pt=pt, e=e: nc.tensor.transpose(pt[:, 0:128], cze[e][:, 2 + c0:2 + c0 + 128], identf[:]),
                           reads=[czeb[e], constb], writes=[pb])
                        vcopy(tmst[j][:, e * 128:(e + 1) * 128], pt[:, 0:128], [pb], [tmstb[j]])
                y = gcs[e]
                ts(y[:, 0:T], cze[e][:, 0:T], wconv[:, e:e + 1], None, ALU.mult, None, [czeb[e], smallb], [gcsb[e]])
                stt(y[:, 0:T], cze[e][:, 1:T + 1], wconv[:, 4 + e:5 + e], y[:, 0:T], ALU.mult, ALU.add,
                    [czeb[e], smallb, gcsb[e]], [gcsb[e]])
                stt(y[:, 0:T], cze[e][:, 2:T + 2], wconv[:, 8 + e:9 + e], y[:, 0:T], ALU.mult, ALU.add,
                    [czeb[e], smallb, gcsb[e]], [gcsb[e]])
                if ps_ == 1:
                    vcopy(czs[:, e, :, 2:10], cze[e][:, 642:770].rearrange("p (s t) -> p s t", t=8), [czeb[e]], [czsb])
                    ts(ysm[:, :, :], czs[:, e, :, 0:8], wconv[:, e:e + 1], None, ALU.mult, None, [czsb, smallb], [ysmb])
                    stt(ysm[:, :, :], czs[:, e, :, 1:9], wconv[:, 4 + e:5 + e], ysm[:, :, :], ALU.mult, ALU.add,
                        [czsb, smallb, ysmb], [ysmb])
                    stt(y[:, 640:768].rearrange("p (s t) -> p s t", t=8), czs[:, e, :, 2:10], wconv[:, 8 + e:9 + e], ysm[:, :, :],
                        ALU.mult, ALU.add, [czsb, smallb, ysmb, gcsb[e]], [gcsb[e]])
            if ps_ == 1:
                dma(SP, cztm_p_d[l], tmst[0][:], tmstd[0], reads=[tmstb[0]])
                dma(SP, cztm_s_d[l], tmst[1][:], tmstd[1], reads=[tmstb[1]])
            sl, sbf = w_get()
            for e in range(4):
                for (c0, c1) in cgroups(lo * 128):
                    pt, pb = pp.next()
                    for dc in range(16):
                        mm(pt[:, 0:c1 - c0], sl[:, dc, e * 128:(e + 1) * 128], actT[:, dc, c0:c1],
                           dc == 0, dc == 15, [sbf] + actbufs(c0, c1), pb, dc == 0)
                    tt(mixT[:, 12 + e, c0:c1], pt[:, 0:c1 - c0], gcs[e][:, c0:c1], ALU.mult,
                       [pb, gcsb[e]], mixbufs(12 + e, c0, c1))

        def phase_b(l, ps_, lo):
            if ps_ == 1:
                switch([stpb])
                dma(SP, stp, st_pool[l].rearrange("(h r) c -> r h c", r=120), st_ds, writes=[stpb])
                dma(SP, stc, st_conv[l], st_ds)
                stpb.w = ('d', st_ds, st_ds.count)
                dma(SP, hist_s_d[l], st_pool[l].rearrange("(s r) c -> s r c", r=15)[:, 8:15, :], hist_ds)
            cur = {}

            def u_part(h):
                e = h % 4
                if e == 0:
                    cur['sl'], cur['sbf'] = w_get()
                sl, sbf = cur['sl'], cur['sbf']
                u_, ub_ = ug[h % 2], ugb[h % 2]
                for (c0, c1) in cgroups(lo * 128):
                    pt, pb = pp.next()
                    for dc in range(16):
                        mm(pt[:, 0:c1 - c0], sl[:, dc, e * 128:(e + 1) * 128], actT[:, dc, c0:c1],
                           dc == 0, dc == 15, [sbf] + actbufs(c0, c1), pb, dc == 0)
                    act(u_[:, c0:c1], pt[:, 0:c1 - c0], AF.Gelu_apprx_tanh, [pb], [ub_])

            def b_part(h):
                u_, ub_ = ug[h % 2], ugb[h % 2]
                for (t0, t1) in ((lo, min(lo + 4, NT)), (min(lo + 4, NT), NT)):
                    if t0 >= t1:
                        continue
                    pt, pb = pp.next()
                    first = True
                    for t in range(t0, t1):
                        samp = (ps_ == 1 and t == NT - 1)
                        wt_, wtb_ = (WTs, WTsb) if samp else (WT, WTb)
                        bo = 128 if samp else 0
                        o = pt[:, (t - t0) * 128:(t - t0 + 1) * 128]
                        mm(o, vtm[:, t, h * 128:(h + 1) * 128], wt_[:, h, :], True, False, [vtb[t], wtb_], pb, first)
                        first = False
                        mm(o, sel[0:8, h * 128:(h + 1) * 128], bsh[0:8, bo:bo + 128], False, False, [ones1b, bshb], pb, False)
                        mm(o, sel[0:8, h * 128:(h + 1) * 128], bsl[0:8, bo:bo + 128], False, True, [ones1b, bshb], pb, False)
                    c0, c1 = t0 * 128, t1 * 128
                    tt(mixT[:, 4 + h, c0:c1], pt[:, 0:c1 - c0], u_[:, c0:c1], ALU.mult, [pb, ub_], mixbufs(4 + h, c0, c1))

            for h in range(8):
                u_part(h)
                if h >= 1:
                    b_part(h - 1)
            b_part(7)

        def phase_out(l, ps_, tiles):
            switch([lngb, lnbb])
            load_bcast(lng, ln1_g[l:l + 1, :], lngb, ln_ds)
            load_bcast(lnb, ln1_b[l:l + 1, :], lnbb, ln_ds)
            lngb.w = lnbb.w = ('d', ln_ds, ln_ds.count)
            for cb in range(4):
                sl, sbf = w_get()
                for t in tiles:
                    pt, pb = pp.next()
                    for ec in range(16):
                        mm(pt[:, :], mixT[:, ec, t * 128:(t + 1) * 128], sl[:, ec, :], ec == 0, ec == 15,
                           [mixb[ec][t], sbf], pb, ec == 0)
                    xs = X[:, t, cb * 512:(cb + 1) * 512]
                    stt(xs, xs, ALPHA, pt[:, :], ALU.mult, ALU.add, [pb, Xb[t][cb]], [Xb[t][cb]])
                    if cb == 3:
                        pre = ln_inline(tiles, tiles.index(t), True)
            sl, sbf = w_get()

            def ff1b0(t):
                for e in range(4):
                    pt, pb = pp.next()
                    for dc in range(16):
                        mm(pt[:, 0:128], sl[:, dc, e * 128:(e + 1) * 128], actT[:, dc, t * 128:(t + 1) * 128],
                           dc == 0, dc == 15, [sbf, actb[t]], pb, dc == 0)
                    relu2(mixT[:, e, t * 128:(t + 1) * 128], pt[:, 0:128], 128, pb, [mixb[e][t]])

            ln_tail(tiles, pre, True, ff1b0)
            load_bcast(lng, ln2_g[l:l + 1, :], lngb, ln_ds)
            load_bcast(lnb, ln2_b[l:l + 1, :], lnbb, ln_ds)
            lngb.w = lnbb.w = ('d', ln_ds, ln_ds.count)

        def phase_ffn(l, ps_, tiles, last):
            c_lo = tiles[0] * 128
            ncol = T - c_lo
            ngrp = 2 if ncol > 512 else 1
            gw = ncol // ngrp
            for s in range(4):
                for fb in range(4):
                    if s == 0 and fb == 0:
                        continue
                    sl, sbf = w_get()
                    for e in range(4):
                        fc = fb * 4 + e
                        for gi in range(ngrp):
                            c0 = c_lo + gi * gw
                            c1 = c0 + gw
                            pt, pb = pp.next()
                            for dc in range(16):
                                mm(pt[:, 0:gw], sl[:, dc, e * 128:(e + 1) * 128], actT[:, dc, c0:c1],
                                   dc == 0, dc == 15, [sbf] + actbufs(c0, c1), pb, dc == 0)
                            relu2(mixT[:, fc, c0:c1], pt[:, 0:gw], gw, pb, mixbufs(fc, c0, c1))
                for db in range(4):
                    sl, sbf = w_get()
                    for t in tiles:
                        pt, pb = pp.next()
                        for fc in range(16):
                            mm(pt[:, :], mixT[:, fc, t * 128:(t + 1) * 128], sl[:, fc, :], fc == 0, fc == 15,
                               [mixb[fc][t], sbf], pb, fc == 0)
                        xs = X[:, t, db * 512:(db + 1) * 512]
                        if s == 0:
                            stt(xs, xs, ALPHA, pt[:, :], ALU.mult, ALU.add, [pb, Xb[t][db]], [Xb[t][db]])
                        else:
                            tt(xs, pt[:, :], xs, ALU.add, [pb, Xb[t][db]], [Xb[t][db]])
                        if s == 3 and db == 3:
                            pre = ln_inline(tiles, tiles.index(t), not last)
            ln_tail(tiles, pre, not last)

        out_tok = []
        for ps_ in range(npass):
            for t in range(NT):
                dma(SP, X[:, t, :], x_in[ps_ * NT + t], Xds[t], writes=Xb[t])
            for t in range(NT):
                to_featmajor(t)
            for l in range(depth):
                load_params(l, ps_)
                lo = min(l, 3) if (ps_ == 0 and npass == 2) else 0
                loh = max(lo - 1, 0)
                phase_v(l, ps_, lo)
                phase_b(l, ps_, lo)
                phase_a(l, ps_, lo, loh)
                phase_c(l, ps_, lo, loh)
                tiles = list(range(min(l, 3), NT)) if (ps_ == 0 and npass == 2) else list(range(NT))
                phase_out(l, ps_, tiles)
                phase_ffn(l, ps_, tiles, l == depth - 1)
            ytiles = (3, 4, 5) if ps_ == 0 else tuple(range(NT))
            ybase = 0 if ps_ == 0 else 3
            if npass == 1:
                ytiles, ybase = (3, 4, 5), 0
            for i, t in enumerate(ytiles):
                if ps_ == 0:
                    yi = i
                else:
                    yi = ybase + i
                dma(SP, y_d[yi], X[:, t, :], Xds[t], reads=Xb[t])

        for ds in Xds + tmstd + [gv_ds, hist_ds]:
            if ds.count:
                SP.wait_for(('d', ds, ds.count))
    return nc


def make_in_maps(inp, depth=DEPTH):
    f32 = np.float32
    xp, xs = inp['x_prompt'], inp['x_sample']
    ident = np.eye(128, dtype=f32)
    jj, ii = np.meshgrid(np.arange(128), np.arange(128), indexing='ij')
    tril = (jj <= ii).astype(f32)
    blk = ((jj // 8 == ii // 8) & (jj <= ii)).astype(f32)
    w_s = inp['w_s'][:depth]
    wsT = np.ascontiguousarray(w_s.transpose(0, 1, 3, 2))
    wsTp = np.ascontiguousarray(np.tile(w_s[:, :, :8, :8].transpose(0, 1, 3, 2), (1, 1, 16, 16)))
    bs = np.ascontiguousarray(inp['b_s'][:depth])
    bss = np.ascontiguousarray(np.tile(inp['b_s'][:depth, :, :8], (1, 1, 16)))
    sel = np.zeros((8, 8, 128), f32)
    for h in range(8):
        sel[h, h, :] = 1.0
    sel = sel.reshape(8, 1024)
    pscale = np.ascontiguousarray(inp['pool_scale'][:depth].reshape(depth, 4, 128).transpose(0, 2, 1))
    wconv = np.ascontiguousarray(inp['w_conv'][:depth].reshape(depth, 3, 4, 128).transpose(0, 3, 1, 2).reshape(depth, 128, 12))
    shared = {
        'w_in': inp['w_in'][:depth], 'w_out': inp['w_out'][:depth], 'w_ff1': inp['w_ff1'][:depth], 'w_ff2': inp['w_ff2'][:depth],
        'w_pool': inp['w_pool'][:depth], 'pscale': pscale, 'wconv': wconv,
        'lnv_g': inp['ln_v_g'][:depth], 'lnv_b': inp['ln_v_b'][:depth], 'wsT': wsT, 'wsTp': wsTp, 'bs': bs, 'bss': bss,
        'ln1_g': inp['ln1_g'][:depth], 'ln1_b': inp['ln1_b'][:depth], 'ln2_g': inp['ln2_g'][:depth], 'ln2_b': inp['ln2_b'][:depth],
        'ident': ident, 'tril': tril, 'blk': blk, 'sel': sel,
    }
    shared = {k: np.ascontiguousarray(v, dtype=f32) for k, v in shared.items()}
    maps = []
    for c in range(8):
        seq, half = c // 2, c % 2
        xt = np.zeros((12, 128, D), f32)
        if half == 1:
            xt[0:3] = xp[seq, 640:1024].reshape(3, 128, D)
        xt[3:11] = xp[seq, half * 1024:(half + 1) * 1024].reshape(8, 128, D)
        xt[11] = xs[c * 16:(c + 1) * 16].reshape(128, D)
        rc = np.zeros((4, 16), f32)
        for g, w in enumerate(WINDOWS):
            for j in range(16):
                rc[g, j] = 1.0 / (min(w, j + 1) if half == 0 else w)
        m = dict(shared)
        m['x_in'] = xt
        m['flag'] = np.full((128, 1), float(half), f32)
        m['rc0'] = np.ascontiguousarray(np.broadcast_to(rc.reshape(1, 64), (128, 64)))
        m['st_pool'] = np.ascontiguousarray(inp['state_pool'][:depth, c * 16:(c + 1) * 16].reshape(depth, 240, 512))
        m['st_conv'] = np.ascontiguousarray(inp['state_conv'][:depth, c * 16:(c + 1) * 16].reshape(depth, 32, 512))
        maps.append(m)
    return maps


def assemble(results, depth=DEPTH):
    f32 = np.float32
    y_prompt = np.zeros((4, 2048, D), f32)
    y_sample = np.zeros((128, 8, D), f32)
    pool_p = np.zeros((depth, 4, 15, 512), f32)
    conv_p = np.zeros((depth, 4, 2, 512), f32)
    pool_s = np.zeros((depth, 128, 15, 512), f32)
    conv_s = np.zeros((depth, 128, 2, 512), f32)
    chunk_v = np.zeros((depth, 128, 8, 1024), f32)
    for c, r in enumerate(results):
        seq, half = c // 2, c % 2
        y_prompt[seq, half * 1024:(half + 1) * 1024] = r['y'][0:8].reshape(1024, D)
        y_sample[c * 16:(c + 1) * 16] = r['y'][8].reshape(16, 8, D)
        sl = slice(c * 16, (c + 1) * 16)
        if half == 1:
            pool_p[:, seq] = r['ptm_p'][:, 113:128]
            conv_p[:, seq] = r['cztm_p'][:, 126:128]
        pool_s[:, sl, 0:7] = r['hist_s']
        pool_s[:, sl, 7:15] = r['ptm_s'].reshape(depth, 16, 8, 512)
        conv_s[:, sl] = r['cztm_s'].reshape(depth, 16, 8, 512)[:, :, 6:8]
        chunk_v[:, sl] = r['chunkv'].reshape(depth, 16, 8, 1024)
    return (y_prompt, y_sample, pool_p, conv_p, pool_s, conv_s, chunk_v)


def kernel(**inputs):
    inp = {k: np.asarray(v) for k, v in inputs.items()}
    nc = build(DEPTH, 2)
    maps = make_in_maps(inp, DEPTH)
    res = run_bass_kernel_spmd(nc, maps, core_ids=list(range(8)))
    return assemble(res.results, DEPTH)
```

```python
import bisect
from contextlib import ExitStack

import numpy as np
import concourse.bass as bass
import concourse.mybir as mybir
from concourse.bass_utils import run_bass_kernel_spmd

F32 = mybir.dt.float32
BF16 = mybir.dt.bfloat16
AF = mybir.ActivationFunctionType
ALU = mybir.AluOpType

DEPTH = 4
D = 2048
NT = 6
T = NT * 128
TG = 384
ALPHA = (2 * DEPTH) ** 0.25
EPS = 1e-5
WINDOWS = (2, 4, 8, 16)
NSLOT = 2


class Eng:
    def __init__(self, h, sem, name):
        self.h, self.sem, self.name = h, sem, name
        self.instrs = []
        self.inc_idx = []
        self.inc_val = {}
        self.n_inc = 0
        self.waited = {}
        self.eager = False

    def ensure_inc(self, idx):
        k = bisect.bisect_left(self.inc_idx, idx)
        if k < len(self.inc_idx):
            return self.inc_val[self.inc_idx[k]]
        self.n_inc += 1
        self.instrs[idx].then_inc(self.sem, 1)
        self.inc_idx.append(idx)
        self.inc_val[idx] = self.n_inc
        return self.n_inc

    def wait_for(self, dep):
        if dep[0] == 'e':
            eng, idx = dep[1], dep[2]
            sem, val, key = eng.sem, eng.ensure_inc(idx), eng.name
        else:
            ds, val = dep[1], dep[2]
            sem, key = ds.sem, ds.name
        if self.waited.get(key, 0) >= val:
            return
        self.h.wait_ge(sem, val)
        self.waited[key] = val


class DSem:
    def __init__(self, sem, name):
        self.sem, self.name, self.count = sem, name, 0


class Buf:
    __slots__ = ('name', 'w', 'r')

    def __init__(self, name=''):
        self.name, self.w, self.r = name, None, {}


def _deps(reads, writes, acc):
    deps = []
    for b in reads:
        if b.w is not None:
            deps.append(b.w)
    if not acc:
        for b in writes:
            if b.r:
                deps.extend(b.r.values())
            elif b.w is not None:
                deps.append(b.w)
    return deps


def op(eng, fn, reads=(), writes=(), acc=False, force_inc=False):
    for d in _deps(reads, writes, acc):
        eng.wait_for(d)
    ins = fn()
    idx = len(eng.instrs)
    eng.instrs.append(ins)
    if eng.eager or force_inc:
        eng.ensure_inc(idx)
    tok = ('e', eng, idx)
    for b in reads:
        b.r[eng.name] = tok
    for b in writes:
        b.w = tok
        if not acc:
            b.r = {}
    return tok


def dma(q, out, in_, ds, reads=(), writes=()):
    for d in _deps(reads, writes, False):
        q.wait_for(d)
    q.h.dma_start(out=out, in_=in_).then_inc(ds.sem, 16)
    ds.count += 16
    tok = ('d', ds, ds.count)
    for b in reads:
        b.r[ds.name] = tok
    for b in writes:
        b.w = tok
        b.r = {}
    return tok


def _tok_key(tok):
    return tok[1].name


def _tok_later(a, b):
    return a if a[2] >= b[2] else b


def inherit(news, olds):
    merged = {}
    for o in olds:
        toks = list(o.r.values())
        if o.w is not None:
            toks.append(o.w)
        for t in toks:
            k = _tok_key(t)
            merged[k] = _tok_later(merged[k], t) if k in merged else t
    for n in news:
        n.r = dict(merged)
        n.w = None


class Rot:
    def __init__(self, items):
        self.items, self.i = items, 0

    def next(self):
        it = self.items[self.i % len(self.items)]
        self.i += 1
        return it


def build(depth=DEPTH, npass=2):
    nc = bass.Bass("TRN2", target_bir_lowering=False)

    def din(name, shape):
        return nc.dram_tensor(name, list(shape), F32, kind="ExternalInput").ap()

    def dout(name, shape):
        return nc.dram_tensor(name, list(shape), F32, kind="ExternalOutput").ap()

    x_in = din("x_in", [12, 128, D])
    flag_d = din("flag", [128, 1])
    rc0_d = din("rc0", [128, 64])
    st_pool = din("st_pool", [depth, 240, 512])
    st_conv = din("st_conv", [depth, 32, 512])
    w_in = din("w_in", [depth, D, 4096])
    w_out = din("w_out", [depth, D, D])
    w_ff1 = din("w_ff1", [depth, D, 8192])
    w_ff2 = din("w_ff2", [depth, 8192, D])
    w_pool = din("w_pool", [depth, 4, 128, 128])
    pscale_d = din("pscale", [depth, 128, 4])
    wconv_d = din("wconv", [depth, 128, 12])
    lnv_g = din("lnv_g", [depth, 1024])
    lnv_b = din("lnv_b", [depth, 1024])
    wsT_d = din("wsT", [depth, 8, 128, 128])
    wsTp_d = din("wsTp", [depth, 8, 128, 128])
    bs_d = din("bs", [depth, 8, 128])
    bss_d = din("bss", [depth, 8, 128])
    sel_d = din("sel", [8, 1024])
    ln1_g = din("ln1_g", [depth, D])
    ln1_b = din("ln1_b", [depth, D])
    ln2_g = din("ln2_g", [depth, D])
    ln2_b = din("ln2_b", [depth, D])
    ident_d = din("ident", [128, 128])
    tril_d = din("tril", [128, 128])
    blk_d = din("blk", [128, 128])

    y_d = dout("y", [9, 128, D])
    ptm_p_d = dout("ptm_p", [depth, 128, 512])
    cztm_p_d = dout("cztm_p", [depth, 128, 512])
    ptm_s_d = dout("ptm_s", [depth, 128, 512])
    cztm_s_d = dout("cztm_s", [depth, 128, 512])
    hist_s_d = dout("hist_s", [depth, 16, 7, 512])
    chunkv_d = dout("chunkv", [depth, 128, 1024])

    with ExitStack() as es:
        def sb(name, shape, dt=F32):
            return es.enter_context(nc.sbuf_tensor(name, list(shape), dt))

        def ps(name, shape, dt=F32):
            return es.enter_context(nc.psum_tensor(name, list(shape), dt))

        def sem(name):
            return es.enter_context(nc.semaphore(name))

        PE = Eng(nc.tensor, sem("s_pe"), "pe")
        ACT = Eng(nc.scalar, sem("s_act"), "act")
        DVE = Eng(nc.vector, sem("s_dve"), "dve")
        POOL = Eng(nc.gpsimd, sem("s_pool"), "pool")
        SP = Eng(nc.sync, sem("s_sp"), "sp")
        ACT.eager = True
        DVE.eager = True

        def dsem(name):
            return DSem(sem(name), name)

        X = sb("X", [128, NT, D])
        Xb = [[Buf() for _ in range(4)] for _ in range(NT)]
        Xds = [dsem(f"xds{t}") for t in range(NT)]
        actT = sb("actT", [128, 16, T], BF16)
        actb = [Buf() for _ in range(NT)]
        mixT = sb("mixT", [128, 16, T], BF16)
        mixb = [[Buf() for _ in range(NT)] for _ in range(16)]
        vtm = sb("vtm", [128, NT, 1024], BF16)
        vtb = [Buf() for _ in range(NT)]
        wsl = [sb(f"wsl{i}", [128, 16, 512], BF16) for i in range(NSLOT)]
        wslb = [Buf() for _ in range(NSLOT)]
        wsld = [dsem(f"wsld{i}") for i in range(NSLOT)]
        R = sb("R", [128, 6160])
        xb16 = [sb(f"xb16_{i}", [128, D], BF16) for i in range(2)]
        xb16b = [Buf(), Buf()]
        dd = sb("dd", [128, T], BF16)
        ddb = Buf()
        ug = [sb(f"ug{i}", [128, T], BF16) for i in range(2)]
        ugb = [Buf() for _ in range(2)]
        rl = [sb(f"rl{i}", [128, 512]) for i in range(2)]
        rlb = [Buf() for _ in range(2)]
        wst = sb("wst", [128, 8, 128], BF16)
        wstb = Buf()
        WT = sb("WT", [128, 8, 128], BF16)
        WTb = Buf()
        WTs = sb("WTs", [128, 8, 128], BF16)
        WTsb = Buf()
        bsf = sb("bsf", [8, 256])
        bsfb = Buf()
        bsh = sb("bsh", [8, 256], BF16)
        bsl = sb("bsl", [8, 256], BF16)
        bshb = Buf()
        sel = sb("sel_t", [8, 1024], BF16)
        ones1b = Buf()
        wpool = sb("wpool_t", [128, 4, 128], BF16)
        wpoolb = Buf()
        wpoold = dsem("wpoold")
        pscale = sb("pscale_t", [128, 4])
        wconv = sb("wconv_t", [128, 12])
        smallb = Buf()
        identf = sb("identf", [128, 128])
        identb = sb("identb", [128, 128], BF16)
        tril = sb("tril_t", [128, 128])
        blk = sb("blk_t", [128, 128])
        flag = sb("flagt", [128, 1])
        rc0 = sb("rc0t", [128, 64])
        epst = sb("epst", [128, 1])
        constb = Buf()
        identbb = Buf()
        hsave_p = sb("hsave_p", [128, depth * 4, 15])
        hsave_c = sb("hsave_c", [128, depth * 4, 2])
        hsb = [[Buf() for _ in range(8)] for _ in range(depth)]
        stats = [sb(f"stats{i}", [128, 4, 6]) for i in range(3)]
        statb = [[Buf() for _ in range(4)] for _ in range(3)]
        mvs = [sb(f"mv{i}", [128, 4]) for i in range(3)]
        mvb = [Buf() for _ in range(3)]
        ln_rot = Rot([0, 1, 2])
        mvall = sb("mvall", [128, NT, 4])
        mvallb = [Buf() for _ in range(NT)]
        tmst = [sb(f"tmst{i}", [128, 512]) for i in range(2)]
        tmstb = [Buf() for _ in range(2)]
        tmstd = [dsem(f"tmstd{i}") for i in range(2)]
        stpb = Buf()
        pes = [sb(f"pes{i}", [128, 16, 23]) for i in range(3)]
        pesb = [Buf() for _ in range(3)]
        czs = sb("czs", [128, 4, 16, 10])
        czsb = Buf()
        ysm = sb("ysm", [128, 16, 8])
        ysmb = Buf()
        tmp15 = sb("tmp15", [128, 16])
        tmp15b = Buf()
        wst_ds = dsem("wst_ds")
        bs_ds = dsem("bs_ds")
        st_ds = dsem("st_ds")
        small_ds = dsem("small_ds")
        hist_ds = dsem("hist_ds")
        ln_ds = dsem("ln_ds")
        cst_ds = dsem("cst_ds")
        sel_ds = dsem("sel_ds")
        gv_ds = dsem("gv_ds")

        gv = [R[:, 0:1024], R[:, 3072:4096]]
        gvb = [Buf(), Buf()]
        vg, vbeta = R[:, 1024:2048], R[:, 2048:3072]
        vgb, vbb = Buf(), Buf()
        pe_v, sa_v, sbb_v = R[:, 0:783], R[:, 800:1583], R[:, 1600:2383]
        peb, sab, sbbb = Buf(), Buf(), Buf()
        gcs = [R[:, e * 768:(e + 1) * 768] for e in range(4)]
        gcsb = [Buf() for _ in range(4)]
        cze = [R[:, 3072 + e * 770:3072 + (e + 1) * 770] for e in range(4)]
        czeb = [Buf() for _ in range(4)]
        stp = R[0:120, 2400:3424].rearrange("p (h c) -> p h c", h=2)
        stc = R[0:32, 3424:3936]
        lng, lnb = R[:, 0:2048], R[:, 2048:4096]
        lngb, lnbb = Buf(), Buf()
        allR = gvb + [vgb, vbb, peb, sab, sbbb, stpb] + gcsb + czeb + [lngb, lnbb]

        def switch(news):
            inherit(news, allR)

        pbank = [ps(f"pb{i}", [128, 512]) for i in range(6)]
        pp = Rot([(pbank[i], Buf()) for i in range(6)])
        ptbank = [ps(f"ptb{i}", [128, 8, 128], BF16) for i in range(2)]
        ptp = Rot([(ptbank[i], Buf()) for i in range(2)])

        blocks = []
        for p_ in range(npass):
            for l in range(depth):
                wi = w_in[l].rearrange("(dc p) c -> p dc c", p=128)
                for c0 in (1536, 2048, 512, 1024, 0, 3072, 3584, 2560):
                    blocks.append(wi[:, :, c0:c0 + 512])
                wo = w_out[l].rearrange("(dc p) c -> p dc c", p=128)
                for cb in range(4):
                    blocks.append(wo[:, :, cb * 512:(cb + 1) * 512])
                w1 = w_ff1[l].rearrange("(dc p) c -> p dc c", p=128)
                w2 = w_ff2[l].rearrange("(fc p) c -> p fc c", p=128)
                for s in range(4):
                    for fb in range(4):
                        c0 = s * 2048 + fb * 512
                        blocks.append(w1[:, :, c0:c0 + 512])
                    for db in range(4):
                        blocks.append(w2[:, s * 16:(s + 1) * 16, db * 512:(db + 1) * 512])
        wstate = {'issue': 0, 'use': 0}

        def w_issue_upto(k):
            while wstate['issue'] <= min(k, len(blocks) - 1):
                b = wstate['issue']
                s = b % NSLOT
                dma(POOL, wsl[s][:], blocks[b], wsld[s], writes=[wslb[s]])
                wstate['issue'] += 1

        def w_get(hold=0):
            b = wstate['use']
            w_issue_upto(b + NSLOT - 1 - hold)
            wstate['use'] += 1
            return wsl[b % NSLOT], wslb[b % NSLOT]

        def mm(out, lhsT, rhs, start, stop, reads, bank, first):
            return op(PE, lambda: nc.tensor.matmul(out, lhsT, rhs, start=start, stop=stop),
                      reads=reads, writes=[bank], acc=not first, force_inc=bool(stop))

        def act(out, in_, func, reads, writes, bias=None, scale=None):
            kw = {}
            if bias is not None:
                kw['bias'] = bias
            if scale is not None:
                kw['scale'] = scale
            return op(ACT, lambda: nc.scalar.activation(out=out, in_=in_, func=func, **kw), reads=reads, writes=writes)

        def tt(out, in0, in1, o, reads, writes):
            return op(DVE, lambda: nc.vector.tensor_tensor(out=out, in0=in0, in1=in1, op=o), reads=reads, writes=writes)

        def ts(out, in0, s1, s2, o0, o1, reads, writes):
            if s2 is None:
                return op(DVE, lambda: nc.vector.tensor_scalar(out=out, in0=in0, scalar1=s1, scalar2=None, op0=o0),
                          reads=reads, writes=writes)
            return op(DVE, lambda: nc.vector.tensor_scalar(out=out, in0=in0, scalar1=s1, scalar2=s2, op0=o0, op1=o1),
                      reads=reads, writes=writes)

        def stt(out, in0, scalar, in1, o0, o1, reads, writes):
            return op(DVE, lambda: nc.vector.scalar_tensor_tensor(out=out, in0=in0, scalar=scalar, in1=in1, op0=o0, op1=o1),
                      reads=reads, writes=writes)

        def pstt(out, in0, scalar, in1, o0, o1, reads, writes):
            return op(POOL, lambda: nc.gpsimd.scalar_tensor_tensor(out=out, in0=in0, scalar=scalar, in1=in1, op0=o0, op1=o1),
                      reads=reads, writes=writes)

        def ptt(out, in0, in1, o, reads, writes):
            return op(POOL, lambda: nc.gpsimd.tensor_tensor(out=out, in0=in0, in1=in1, op=o), reads=reads, writes=writes)

        def pts(out, in0, s1, o0, reads, writes):
            return op(POOL, lambda: nc.gpsimd.tensor_scalar(out=out, in0=in0, scalar1=s1, scalar2=None, op0=o0),
                      reads=reads, writes=writes)

        rl_rot = Rot([0, 1])

        def relu2(out, psum_ap, n, pb, wbufs):
            ri = rl_rot.next()
            act(rl[ri][:, 0:n], psum_ap, AF.Relu, [pb], [rlb[ri]])
            act(out, rl[ri][:, 0:n], AF.Square, [rlb[ri]], wbufs)

        def vcopy(out, in_, reads, writes):
            return op(DVE, lambda: nc.vector.tensor_copy(out=out, in_=in_), reads=reads, writes=writes)

        def vmemset(ap, val, writes):
            return op(DVE, lambda: nc.vector.memset(ap, val), writes=writes)

        def cgroups(c_lo):
            ncol = T - c_lo
            ngrp = 2 if ncol > 512 else 1
            gw = ncol // ngrp
            return [(c_lo + i * gw, c_lo + (i + 1) * gw) for i in range(ngrp)]

        def tiles_of(c0, c1):
            return range(c0 // 128, (c1 + 127) // 128)

        def mixbufs(e, c0, c1):
            return [mixb[e][t] for t in tiles_of(c0, c1)]

        def actbufs(c0, c1):
            return [actb[t] for t in tiles_of(c0, c1)]

        def layernorm(xap, xbufs, W, g_ap, b_ap, gbuf, bbuf):
            nk = W // 512
            i = ln_rot.next()
            st, stb, mv, mb = stats[i], statb[i], mvs[i], mvb[i]
            for k in range(nk):
                op(DVE, lambda k=k: nc.vector.bn_stats(out=st[:, k, :], in_=xap[:, k * 512:(k + 1) * 512]),
                   reads=[xbufs[k]], writes=[stb[k]])
            op(DVE, lambda: nc.vector.bn_aggr(out=mv[:, 0:2], in_=st[:, 0:nk, :].rearrange("p k s -> p (k s)")),
               reads=stb[0:nk], writes=[mb])
            act(mv[:, 2:3], mv[:, 1:2], AF.Sqrt, reads=[mb, constb], writes=[mb], bias=epst[:, 0:1], scale=1.0)
            op(DVE, lambda: nc.vector.reciprocal(out=mv[:, 2:3], in_=mv[:, 2:3]), reads=[mb], writes=[mb])
            stt(xap[:, 0:W], xap[:, 0:W], mv[:, 0:1], g_ap[:, 0:W], ALU.subtract, ALU.mult,
                list(xbufs[0:nk]) + [mb, gbuf], list(xbufs[0:nk]))
            stt(xap[:, 0:W], xap[:, 0:W], mv[:, 2:3], b_ap[:, 0:W], ALU.mult, ALU.add,
                list(xbufs[0:nk]) + [mb, bbuf], list(xbufs[0:nk]))

        def ln_stats(t):
            i = ln_rot.next()
            st, stb = stats[i], statb[i]
            for k in range(4):
                op(DVE, lambda k=k: nc.vector.bn_stats(out=st[:, k, :], in_=X[:, t, k * 512:(k + 1) * 512]),
                   reads=[Xb[t][k]], writes=[stb[k]])
            op(DVE, lambda: nc.vector.bn_aggr(out=mvall[:, t, 0:2], in_=st[:, 0:4, :].rearrange("p k s -> p (k s)")),
               reads=stb[0:4], writes=[mvallb[t]])

        def ln_rstd(tiles):
            t0, t1 = tiles[0], tiles[-1] + 1
            act(mvall[:, t0:t1, 2], mvall[:, t0:t1, 1], AF.Sqrt, reads=mvallb[t0:t1] + [constb], writes=mvallb[t0:t1],
                bias=epst[:, 0:1], scale=1.0)
            op(DVE, lambda: nc.vector.reciprocal(out=mvall[:, t0:t1, 2], in_=mvall[:, t0:t1, 2]),
               reads=mvallb[t0:t1], writes=mvallb[t0:t1])

        def ln_apply(t):
            stt(X[:, t, :], X[:, t, :], mvall[:, t, 0:1], lng[:, :], ALU.subtract, ALU.mult,
                Xb[t] + [mvallb[t], lngb], Xb[t])
            stt(X[:, t, :], X[:, t, :], mvall[:, t, 2:3], lnb[:, :], ALU.mult, ALU.add,
                Xb[t] + [mvallb[t], lnbb], Xb[t])

        def fm_cast(t):
            act(xb16[t % 2][:, :], X[:, t, :], AF.Copy, reads=Xb[t], writes=[xb16b[t % 2]])

        def fm_tr(t):
            xb, xbb = xb16[t % 2], xb16b[t % 2]
            for half in range(2):
                pt, pb = ptp.next()
                for k in range(8):
                    dc = half * 8 + k
                    op(PE, lambda k=k, dc=dc, pt=pt: nc.tensor.transpose(pt[:, k, :], xb[:, dc * 128:(dc + 1) * 128], identb[:]),
                       reads=[xbb, identbb], writes=[pb], acc=(k > 0), force_inc=(k == 7))
                act(actT[:, half * 8:(half + 1) * 8, t * 128:(t + 1) * 128], pt[:, :, :], AF.Copy, [pb], [actb[t]])

        def to_featmajor(t):
            fm_cast(t)
            fm_tr(t)

        def ln_tail(tiles, pre_done, do_fm, ff1b0=None):
            n = len(tiles)
            for i in range(n + 2):
                if i < n and i >= pre_done:
                    ln_apply(tiles[i])
                    if do_fm:
                        fm_cast(tiles[i])
                if do_fm and 1 <= i <= n:
                    fm_tr(tiles[i - 1])
                if ff1b0 is not None and 2 <= i <= n + 1:
                    ff1b0(tiles[i - 2])

        def ln_inline(tiles, i, do_fm):
            t = tiles[i]
            ln_stats(t)
            n = len(tiles)
            if n >= 4 and i == 2:
                ln_rstd(tiles[0:3])
                ln_apply(tiles[0])
                if do_fm:
                    fm_cast(tiles[0])
            if i == n - 1:
                ln_rstd(tiles[3:] if n >= 4 else tiles)
            return 1 if n >= 4 else 0

        def load_bcast(dst, src_row, buf, ds):
            dma(SP, dst, src_row.partition_broadcast(128), ds, writes=[buf])

        for dst, src in ((identf, ident_d), (tril, tril_d), (blk, blk_d), (flag, flag_d), (rc0, rc0_d)):
            dma(SP, dst[:], src, cst_ds)
        constb.w = ('d', cst_ds, cst_ds.count)
        vmemset(epst[:], EPS, [constb])
        dma(POOL, sel[:], sel_d, sel_ds, writes=[ones1b])
        act(identb[:], identf[:], AF.Copy, reads=[constb], writes=[identbb])

        def load_params(l, ps_):
            dma(POOL, wst[:], wsT_d[l].rearrange("h j i -> j h i"), wst_ds, writes=[wstb])
            for h in range(8):
                tt(WT[:, h, :], wst[:, h, :], tril[:], ALU.mult, [wstb, constb], [WTb])
            if ps_ == 1:
                dma(POOL, wst[:], wsTp_d[l].rearrange("h j i -> j h i"), wst_ds, writes=[wstb])
                for h in range(8):
                    tt(WTs[:, h, :], wst[:, h, :], blk[:], ALU.mult, [wstb, constb], [WTsb])
            dma(SP, bsf[:, 0:128], bs_d[l], bs_ds, writes=[bsfb])
            dma(SP, bsf[:, 128:256], bss_d[l], bs_ds)
            bsfb.w = ('d', bs_ds, bs_ds.count)
            vcopy(bsh[:], bsf[:], [bsfb], [bshb])
            tt(bsl[:], bsf[:], bsh[:], ALU.subtract, [bsfb, bshb], [bshb])
            dma(POOL, wpool[:], w_pool[l].rearrange("g c e -> c g e"), wpoold, writes=[wpoolb])
            dma(SP, pscale[:], pscale_d[l], small_ds, writes=[smallb])
            dma(SP, wconv[:], wconv_d[l], small_ds)
            smallb.w = ('d', small_ds, small_ds.count)

        def phase_v(l, ps_, lo):
            switch(gvb + [vgb, vbb])
            load_bcast(vg, lnv_g[l:l + 1, :], vgb, ln_ds)
            load_bcast(vbeta, lnv_b[l:l + 1, :], vbb, ln_ds)
            vgb.w = vbb.w = ('d', ln_ds, ln_ds.count)
            s0, b0 = w_get()
            s1, b1 = w_get(hold=1)
            slots = ((s0, b0), (s1, b1))

            def v_mm(t, vbk):
                sl, sbf = slots[vbk]
                pt, pb = pp.next()
                for dc in range(16):
                    mm(pt[:, :], actT[:, dc, t * 128:(t + 1) * 128], sl[:, dc, :], dc == 0, dc == 15,
                       [actb[t], sbf], pb, dc == 0)
                return pt, pb

            def v_post(t, halves):
                g_ = gv[t % 2]
                gb_ = gvb[t % 2]
                for vbk, (pt, pb) in enumerate(halves):
                    act(g_[:, vbk * 512:(vbk + 1) * 512], pt[:, :], AF.Gelu_apprx_tanh, [pb], [gb_])
                layernorm(g_, [gb_, gb_], 1024, vg, vbeta, vgb, vbb)
                if ps_ == 1 and t == NT - 1:
                    dma(SP, chunkv_d[l], g_, gv_ds, reads=[gb_])
                act(vtm[:, t, :], g_, AF.Copy, [gb_], [vtb[t]])

            vt = list(range(lo, NT))
            for t in vt[:-2]:
                v_post(t, [v_mm(t, 0), v_mm(t, 1)])
            ta, tb = vt[-2], vt[-1]
            a0 = v_mm(ta, 0)
            b0_ = v_mm(tb, 0)
            a1 = v_mm(ta, 1)
            b1_ = v_mm(tb, 1)
            v_post(ta, [a0, a1])
            v_post(tb, [b0_, b1_])

        def phase_a(l, ps_, tlo, tloh):
            switch([peb, sab, sbbb])
            sl, sbf = w_get()
            if ps_ == 1:
                for e in range(4):
                    pt, pb = pp.next()
                    op(PE, lambda pt=pt, e=e: nc.tensor.transpose(pt[:, 0:32], stc[0:32, e * 128:(e + 1) * 128], identf[0:32, 0:32]),
                       reads=[stpb, constb], writes=[pb])
                    vcopy(czs[:, e, :, 0:2], pt[:, 0:32].rearrange("p (s r) -> p s r", r=2), [pb], [czsb])
            for g in range(4):
                w = WINDOWS[g]
                for (c0, c1) in cgroups(tloh * 128):
                    pt, pb = pp.next()
                    for dc in range(16):
                        mm(pt[:, 0:c1 - c0], sl[:, dc, g * 128:(g + 1) * 128], actT[:, dc, c0:c1],
                           dc == 0, dc == 15, [sbf] + actbufs(c0, c1), pb, dc == 0)
                    act(pe_v[:, 15 + c0:15 + c1], pt[:, 0:c1 - c0], AF.Copy, [pb], [peb])
                hb = hsb[l][g]
                if ps_ == 0:
                    vmemset(pe_v[:, 0:15], 0.0, [peb])
                    ts(pe_v[:, 384:399], pe_v[:, 384:399], flag[:, 0:1], None, ALU.mult, None, [peb, constb], [peb])
                    vcopy(hsave_p[:, l * 4 + g, :], pe_v[:, 768:783], [peb], [hb])
                else:
                    vcopy(pe_v[:, 0:15], hsave_p[:, l * 4 + g, :], [hb], [peb])
                    for j, c0 in enumerate((512, 640)):
                        pt, pb = pp.next()
                        op(PE, lambda c0=c0, pt=pt: nc.tensor.transpose(pt[:, 0:128], pe_v[:, 15 + c0:15 + c0 + 128], identf[:]),
                           reads=[peb, constb], writes=[pb])
                        vcopy(tmst[j][:, g * 128:(g + 1) * 128], pt[:, 0:128], [pb], [tmstb[j]])
                cur, curb = pe_v, peb
                nxts = [(sa_v, sab), (sbb_v, sbbb)]
                for k in range(g + 1):
                    sh = 1 << k
                    lo = (1 << (k + 1)) - 1
                    nx, nxb = nxts[k % 2]
                    tt(nx[:, lo:783], cur[:, lo:783], cur[:, lo - sh:783 - sh], ALU.add, [curb], [nxb])
                    cur, curb = nx, nxb
                stt(dd[:, 0:T], cur[:, 15:783], 1.0 / w, pe_v[:, 15:783], ALU.mult, ALU.subtract, [curb, peb], [ddb])
                if ps_ == 0:
                    tt(tmp15[:, 0:15], cur[:, 399:414], rc0[:, g * 16:g * 16 + 15], ALU.mult, [curb, constb], [tmp15b])
                    tt(dd[:, 384:399], tmp15[:, 0:15], pe_v[:, 399:414], ALU.subtract, [tmp15b, peb], [ddb])
                else:
                    p0, p0b = pes[0], pesb[0]
                    for half in range(2):
                        pt, pb = pp.next()
                        op(PE, lambda half=half, pt=pt: nc.tensor.transpose(pt[:, 0:120], stp[0:120, half, g * 128:(g + 1) * 128],
                                                                           identf[0:120, 0:120]),
                           reads=[stpb, constb], writes=[pb])
                        vcopy(p0[:, half * 8:(half + 1) * 8, 0:15], pt[:, 0:120].rearrange("p (s r) -> p s r", r=15), [pb], [p0b])
                    vcopy(p0[:, :, 15:23], pe_v[:, 655:783].rearrange("p (s t) -> p s t", t=8), [peb], [p0b])
                    c_, cb_ = p0, p0b
                    nx2 = [(pes[1], pesb[1]), (pes[2], pesb[2])]
                    for k in range(g + 1):
                        sh = 1 << k
                        lo = (1 << (k + 1)) - 1
                        nx, nxb = nx2[k % 2]
                        tt(nx[:, :, lo:23], c_[:, :, lo:23], c_[:, :, lo - sh:23 - sh], ALU.add, [cb_], [nxb])
                        c_, cb_ = nx, nxb
                    stt(dd[:, 640:768].rearrange("p (s t) -> p s t", t=8), c_[:, :, 15:23], 1.0 / w, p0[:, :, 15:23],
                        ALU.mult, ALU.subtract, [cb_, p0b], [ddb])
                for (c0, c1) in cgroups(tlo * 128):
                    pt, pb = pp.next()
                    mm(pt[:, 0:c1 - c0], wpool[:, g, :], dd[:, c0:c1], True, True, [wpoolb, ddb], pb, True)
                    act(mixT[:, g, c0:c1], pt[:, 0:c1 - c0], AF.Copy, [pb, smallb],
                        mixbufs(g, c0, c1), scale=pscale[:, g:g + 1])
            if ps_ == 1:
                dma(SP, ptm_p_d[l], tmst[0][:], tmstd[0], reads=[tmstb[0]])
                dma(SP, ptm_s_d[l], tmst[1][:], tmstd[1], reads=[tmstb[1]])

        def phase_c(l, ps_, lo, loh):
            switch(gcsb + czeb)
            sl, sbf = w_get()
            for e in range(4):
                for (c0, c1) in cgroups(loh * 128):
                    pt, pb = pp.next()
                    for dc in range(16):
                        mm(pt[:, 0:c1 - c0], sl[:, dc, e * 128:(e + 1) * 128], actT[:, dc, c0:c1],
                           dc == 0, dc == 15, [sbf] + actbufs(c0, c1), pb, dc == 0)
                    act(gcs[e][:, c0:c1], pt[:, 0:c1 - c0], AF.Copy, [pb], [gcsb[e]])
            sl, sbf = w_get()
            for e in range(4):
                for (c0, c1) in cgroups(loh * 128):
                    pt, pb = pp.next()
                    for dc in range(16):
                        mm(pt[:, 0:c1 - c0], sl[:, dc, e * 128:(e + 1) * 128], actT[:, dc, c0:c1],
                           dc == 0, dc == 15, [sbf] + actbufs(c0, c1), pb, dc == 0)
                    tt(cze[e][:, 2 + c0:2 + c1], pt[:, 0:c1 - c0], gcs[e][:, c0:c1], ALU.mult,
                       [pb, gcsb[e]], [czeb[e]])
                hb = hsb[l][4 + e]
                if ps_ == 0:
                    vmemset(cze[e][:, 0:2], 0.0, [czeb[e]])
                    ts(cze[e][:, 384:386], cze[e][:, 384:386], flag[:, 0:1], None, ALU.mult, None, [czeb[e], constb], [czeb[e]])
                    vcopy(hsave_c[:, l * 4 + e, :], cze[e][:, 768:770], [czeb[e]], [hb])
                else:
                    vcopy(cze[e][:, 0:2], hsave_c[:, l * 4 + e, :], [hb], [czeb[e]])
                    for j, c0 in enumerate((512, 640)):
                        pt, pb = pp.next()
                        op(PE, lambda c0=c0, pt=pt, e=e: nc.tensor.transpose(pt[:, 0:128], cze[e][:, 2 + c0:2 + c0 + 128], identf[:]),
                           reads=[czeb[e], constb], writes=[pb])
                        vcopy(tmst[j][:, e * 128:(e + 1) * 128], pt[:, 0:128], [pb], [tmstb[j]])
                y = gcs[e]
                ts(y[:, 0:T], cze[e][:, 0:T], wconv[:, e:e + 1], None, ALU.mult, None, [czeb[e], smallb], [gcsb[e]])
                stt(y[:, 0:T], cze[e][:, 1:T + 1], wconv[:, 4 + e:5 + e], y[:, 0:T], ALU.mult, ALU.add,
                    [czeb[e], smallb, gcsb[e]], [gcsb[e]])
                stt(y[:, 0:T], cze[e][:, 2:T + 2], wconv[:, 8 + e:9 + e], y[:, 0:T], ALU.mult, ALU.add,
                    [czeb[e], smallb, gcsb[e]], [gcsb[e]])
                if ps_ == 1:
                    vcopy(czs[:, e, :, 2:10], cze[e][:, 642:770].rearrange("p (s t) -> p s t", t=8), [czeb[e]], [czsb])
                    ts(ysm[:, :, :], czs[:, e, :, 0:8], wconv[:, e:e + 1], None, ALU.mult, None, [czsb, smallb], [ysmb])
                    stt(ysm[:, :, :], czs[:, e, :, 1:9], wconv[:, 4 + e:5 + e], ysm[:, :, :], ALU.mult, ALU.add,
                        [czsb, smallb, ysmb], [ysmb])
                    stt(y[:, 640:768].rearrange("p (s t) -> p s t", t=8), czs[:, e, :, 2:10], wconv[:, 8 + e:9 + e], ysm[:, :, :],
                        ALU.mult, ALU.add, [czsb, smallb, ysmb, gcsb[e]], [gcsb[e]])
            if ps_ == 1:
                dma(SP, cztm_p_d[l], tmst[0][:], tmstd[0], reads=[tmstb[0]])
                dma(SP, cztm_s_d[l], tmst[1][:], tmstd[1], reads=[tmstb[1]])
            sl, sbf = w_get()
            for e in range(4):
                for (c0, c1) in cgroups(lo * 128):
                    pt, pb = pp.next()
                    for dc in range(16):
                        mm(pt[:, 0:c1 - c0], sl[:, dc, e * 128:(e + 1) * 128], actT[:, dc, c0:c1],
                           dc == 0, dc == 15, [sbf] + actbufs(c0, c1), pb, dc == 0)
                    tt(mixT[:, 12 + e, c0:c1], pt[:, 0:c1 - c0], gcs[e][:, c0:c1], ALU.mult,
                       [pb, gcsb[e]], mixbufs(12 + e, c0, c1))

        def phase_b(l, ps_, lo):
            if ps_ == 1:
                switch([stpb])
                dma(SP, stp, st_pool[l].rearrange("(h r) c -> r h c", r=120), st_ds, writes=[stpb])
                dma(SP, stc, st_conv[l], st_ds)
                stpb.w = ('d', st_ds, st_ds.count)
                dma(SP, hist_s_d[l], st_pool[l].rearrange("(s r) c -> s r c", r=15)[:, 8:15, :], hist_ds)
            cur = {}

            def u_part(h):
                e = h % 4
                if e == 0:
                    cur['sl'], cur['sbf'] = w_get()
                sl, sbf = cur['sl'], cur['sbf']
                u_, ub_ = ug[h % 2], ugb[h % 2]
                for (c0, c1) in cgroups(lo * 128):
                    pt, pb = pp.next()
                    for dc in range(16):
                        mm(pt[:, 0:c1 - c0], sl[:, dc, e * 128:(e + 1) * 128], actT[:, dc, c0:c1],
                           dc == 0, dc == 15, [sbf] + actbufs(c0, c1), pb, dc == 0)
                    act(u_[:, c0:c1], pt[:, 0:c1 - c0], AF.Gelu_apprx_tanh, [pb], [ub_])

            def b_part(h):
                u_, ub_ = ug[h % 2], ugb[h % 2]
                for (t0, t1) in ((lo, min(lo + 4, NT)), (min(lo + 4, NT), NT)):
                    if t0 >= t1:
                        continue
                    pt, pb = pp.next()
                    first = True
                    for t in range(t0, t1):
                        samp = (ps_ == 1 and t == NT - 1)
                        wt_, wtb_ = (WTs, WTsb) if samp else (WT, WTb)
                        bo = 128 if samp else 0
                        o = pt[:, (t - t0) * 128:(t - t0 + 1) * 128]
                        mm(o, vtm[:, t, h * 128:(h + 1) * 128], wt_[:, h, :], True, False, [vtb[t], wtb_], pb, first)
                        first = False
                        mm(o, sel[0:8, h * 128:(h + 1) * 128], bsh[0:8, bo:bo + 128], False, False, [ones1b, bshb], pb, False)
                        mm(o, sel[0:8, h * 128:(h + 1) * 128], bsl[0:8, bo:bo + 128], False, True, [ones1b, bshb], pb, False)
                    c0, c1 = t0 * 128, t1 * 128
                    tt(mixT[:, 4 + h, c0:c1], pt[:, 0:c1 - c0], u_[:, c0:c1], ALU.mult, [pb, ub_], mixbufs(4 + h, c0, c1))

            for h in range(8):
                u_part(h)
                if h >= 1:
                    b_part(h - 1)
            b_part(7)

        def phase_out(l, ps_, tiles):
            switch([lngb, lnbb])
            load_bcast(lng, ln1_g[l:l + 1, :], lngb, ln_ds)
            load_bcast(lnb, ln1_b[l:l + 1, :], lnbb, ln_ds)
            lngb.w = lnbb.w = ('d', ln_ds, ln_ds.count)
            for cb in range(4):
                sl, sbf = w_get()
                for t in tiles:
                    pt, pb = pp.next()
                    for ec in range(16):
                        mm(pt[:, :], mixT[:, ec, t * 128:(t + 1) * 128], sl[:, ec, :], ec == 0, ec == 15,
                           [mixb[ec][t], sbf], pb, ec == 0)
                    xs = X[:, t, cb * 512:(cb + 1) * 512]
                    stt(xs, xs, ALPHA, pt[:, :], ALU.mult, ALU.add, [pb, Xb[t][cb]], [Xb[t][cb]])
                    if cb == 3:
                        pre = ln_inline(tiles, tiles.index(t), True)
            sl, sbf = w_get()

            def ff1b0(t):
                for e in range(4):
                    pt, pb = pp.next()
                    for dc in range(16):
                        mm(pt[:, 0:128], sl[:, dc, e * 128:(e + 1) * 128], actT[:, dc, t * 128:(t + 1) * 128],
                           dc == 0, dc == 15, [sbf, actb[t]], pb, dc == 0)
                    relu2(mixT[:, e, t * 128:(t + 1) * 128], pt[:, 0:128], 128, pb, [mixb[e][t]])

            ln_tail(tiles, pre, True, ff1b0)
            load_bcast(lng, ln2_g[l:l + 1, :], lngb, ln_ds)
            load_bcast(lnb, ln2_b[l:l + 1, :], lnbb, ln_ds)
            lngb.w = lnbb.w = ('d', ln_ds, ln_ds.count)

        def phase_ffn(l, ps_, tiles, last):
            c_lo = tiles[0] * 128
            ncol = T - c_lo
            ngrp = 2 if ncol > 512 else 1
            gw = ncol // ngrp
            for s in range(4):
                for fb in range(4):
                    if s == 0 and fb == 0:
                        continue
                    sl, sbf = w_get()
                    for e in range(4):
                        fc = fb * 4 + e
                        for gi in range(ngrp):
                            c0 = c_lo + gi * gw
                            c1 = c0 + gw
                            pt, pb = pp.next()
                            for dc in range(16):
                                mm(pt[:, 0:gw], sl[:, dc, e * 128:(e + 1) * 128], actT[:, dc, c0:c1],
                                   dc == 0, dc == 15, [sbf] + actbufs(c0, c1), pb, dc == 0)
                            relu2(mixT[:, fc, c0:c1], pt[:, 0:gw], gw, pb, mixbufs(fc, c0, c1))
                for db in range(4):
                    sl, sbf = w_get()
                    for t in tiles:
                        pt, pb = pp.next()
                        for fc in range(16):
                            mm(pt[:, :], mixT[:, fc, t * 128:(t + 1) * 128], sl[:, fc, :], fc == 0, fc == 15,
                               [mixb[fc][t], sbf], pb, fc == 0)
                        xs = X[:, t, db * 512:(db + 1) * 512]
                        if s == 0:
                            stt(xs, xs, ALPHA, pt[:, :], ALU.mult, ALU.add, [pb, Xb[t][db]], [Xb[t][db]])
                        else:
                            tt(xs, pt[:, :], xs, ALU.add, [pb, Xb[t][db]], [Xb[t][db]])
                        if s == 3 and db == 3:
                            pre = ln_inline(tiles, tiles.index(t), not last)
            ln_tail(tiles, pre, not last)

        out_tok = []
        for ps_ in range(npass):
            for t in range(NT):
                dma(SP, X[:, t, :], x_in[ps_ * NT + t], Xds[t], writes=Xb[t])
            for t in range(NT):
                to_featmajor(t)
            for l in range(depth):
                load_params(l, ps_)
                lo = min(l, 3) if (ps_ == 0 and npass == 2) else 0
                loh = max(lo - 1, 0)
                phase_v(l, ps_, lo)
                phase_b(l, ps_, lo)
                phase_a(l, ps_, lo, loh)
                phase_c(l, ps_, lo, loh)
                tiles = list(range(min(l, 3), NT)) if (ps_ == 0 and npass == 2) else list(range(NT))
                phase_out(l, ps_, tiles)
                phase_ffn(l, ps_, tiles, l == depth - 1)
            ytiles = (3, 4, 5) if ps_ == 0 else tuple(range(NT))
            ybase = 0 if ps_ == 0 else 3
            if npass == 1:
                ytiles, ybase = (3, 4, 5), 0
            for i, t in enumerate(ytiles):
                if ps_ == 0:
                    yi = i
                else:
                    yi = ybase + i
                dma(SP, y_d[yi], X[:, t, :], Xds[t], reads=Xb[t])

        for ds in Xds + tmstd + [gv_ds, hist_ds]:
            if ds.count:
                SP.wait_for(('d', ds, ds.count))
    return nc


def make_in_maps(inp, depth=DEPTH):
    f32 = np.float32
    xp, xs = inp['x_prompt'], inp['x_sample']
    ident = np.eye(128, dtype=f32)
    jj, ii = np.meshgrid(np.arange(128), np.arange(128), indexing='ij')
    tril = (jj <= ii).astype(f32)
    blk = ((jj // 8 == ii // 8) & (jj <= ii)).astype(f32)
    w_s = inp['w_s'][:depth]
    wsT = np.ascontiguousarray(w_s.transpose(0, 1, 3, 2))
    wsTp = np.ascontiguousarray(np.tile(w_s[:, :, :8, :8].transpose(0, 1, 3, 2), (1, 1, 16, 16)))
    bs = np.ascontiguousarray(inp['b_s'][:depth])
    bss = np.ascontiguousarray(np.tile(inp['b_s'][:depth, :, :8], (1, 1, 16)))
    sel = np.zeros((8, 8, 128), f32)
    for h in range(8):
        sel[h, h, :] = 1.0
    sel = sel.reshape(8, 1024)
    pscale = np.ascontiguousarray(inp['pool_scale'][:depth].reshape(depth, 4, 128).transpose(0, 2, 1))
    wconv = np.ascontiguousarray(inp['w_conv'][:depth].reshape(depth, 3, 4, 128).transpose(0, 3, 1, 2).reshape(depth, 128, 12))
    shared = {
        'w_in': inp['w_in'][:depth], 'w_out': inp['w_out'][:depth], 'w_ff1': inp['w_ff1'][:depth], 'w_ff2': inp['w_ff2'][:depth],
        'w_pool': inp['w_pool'][:depth], 'pscale': pscale, 'wconv': wconv,
        'lnv_g': inp['ln_v_g'][:depth], 'lnv_b': inp['ln_v_b'][:depth], 'wsT': wsT, 'wsTp': wsTp, 'bs': bs, 'bss': bss,
        'ln1_g': inp['ln1_g'][:depth], 'ln1_b': inp['ln1_b'][:depth], 'ln2_g': inp['ln2_g'][:depth], 'ln2_b': inp['ln2_b'][:depth],
        'ident': ident, 'tril': tril, 'blk': blk, 'sel': sel,
    }
    shared = {k: np.ascontiguousarray(v, dtype=f32) for k, v in shared.items()}
    maps = []
    for c in range(8):
        seq, half = c // 2, c % 2
        xt = np.zeros((12, 128, D), f32)
        if half == 1:
            xt[0:3] = xp[seq, 640:1024].reshape(3, 128, D)
        xt[3:11] = xp[seq, half * 1024:(half + 1) * 1024].reshape(8, 128, D)
        xt[11] = xs[c * 16:(c + 1) * 16].reshape(128, D)
        rc = np.zeros((4, 16), f32)
        for g, w in enumerate(WINDOWS):
            for j in range(16):
                rc[g, j] = 1.0 / (min(w, j + 1) if half == 0 else w)
        m = dict(shared)
        m['x_in'] = xt
        m['flag'] = np.full((128, 1), float(half), f32)
        m['rc0'] = np.ascontiguousarray(np.broadcast_to(rc.reshape(1, 64), (128, 64)))
        m['st_pool'] = np.ascontiguousarray(inp['state_pool'][:depth, c * 16:(c + 1) * 16].reshape(depth, 240, 512))
        m['st_conv'] = np.ascontiguousarray(inp['state_conv'][:depth, c * 16:(c + 1) * 16].reshape(depth, 32, 512))
        maps.append(m)
    return maps


def assemble(results, depth=DEPTH):
    f32 = np.float32
    y_prompt = np.zeros((4, 2048, D), f32)
    y_sample = np.zeros((128, 8, D), f32)
    pool_p = np.zeros((depth, 4, 15, 512), f32)
    conv_p = np.zeros((depth, 4, 2, 512), f32)
    pool_s = np.zeros((depth, 128, 15, 512), f32)
    conv_s = np.zeros((depth, 128, 2, 512), f32)
    chunk_v = np.zeros((depth, 128, 8, 1024), f32)
    for c, r in enumerate(results):
        seq, half = c // 2, c % 2
        y_prompt[seq, half * 1024:(half + 1) * 1024] = r['y'][0:8].reshape(1024, D)
        y_sample[c * 16:(c + 1) * 16] = r['y'][8].reshape(16, 8, D)
        sl = slice(c * 16, (c + 1) * 16)
        if half == 1:
            pool_p[:, seq] = r['ptm_p'][:, 113:128]
            conv_p[:, seq] = r['cztm_p'][:, 126:128]
        pool_s[:, sl, 0:7] = r['hist_s']
        pool_s[:, sl, 7:15] = r['ptm_s'].reshape(depth, 16, 8, 512)
        conv_s[:, sl] = r['cztm_s'].reshape(depth, 16, 8, 512)[:, :, 6:8]
        chunk_v[:, sl] = r['chunkv'].reshape(depth, 16, 8, 1024)
    return (y_prompt, y_sample, pool_p, conv_p, pool_s, conv_s, chunk_v)


def kernel(**inputs):
    inp = {k: np.asarray(v) for k, v in inputs.items()}
    nc = build(DEPTH, 2)
    maps = make_in_maps(inp, DEPTH)
    res = run_bass_kernel_spmd(nc, maps, core_ids=list(range(8)))
    return assemble(res.results, DEPTH)
```
